# Optimizing a Trainium2 kernel written in Bass

```python
import math
import jax, jax.numpy as jnp
from jax import lax
import numpy as np

D_MODEL = 2048
BATCH = 8
SEQ = 2048
DEPTH = 2

CHUNK = 64
Q_BLOCK = 128
N_MIXERS = 2
N_GLA = (DEPTH + 1) // 2
N_MLA = DEPTH // 2
DEEPNORM_ALPHA = float((2 * DEPTH) ** 0.25)
DEEPNORM_BETA = float((8 * DEPTH) ** -0.25)
LN_EPS = 1e-5
RMS_EPS = 1e-6

GLA_HEADS = 4
GLA_DK = D_MODEL // 2 // GLA_HEADS
GLA_DV = D_MODEL // GLA_HEADS
GLA_HK = GLA_HEADS * GLA_DK
GLA_HV = GLA_HEADS * GLA_DV
GLA_GATE_RANK = 16
GLA_TAU = 16.0
GLA_IN = 2 * GLA_HK + 2 * GLA_HV + GLA_GATE_RANK

MLA_HEADS = 16
MLA_Q_RANK = D_MODEL // 4
MLA_KV_RANK = D_MODEL // 4
MLA_NOPE = 128
MLA_ROPE = 64
MLA_V = 128
MLA_QK = MLA_NOPE + MLA_ROPE
MLA_HV = MLA_HEADS * MLA_V
MLA_IN = MLA_Q_RANK + MLA_KV_RANK + MLA_ROPE + MLA_HV
ROPE_THETA = 10000.0

kernel_name = "hybrid_gla_mla_deepnorm_stream"


def layer_norm(x, g, b):
    xf = x.astype(jnp.float32)
    mu = jnp.mean(xf, axis=-1, keepdims=True)
    xc = xf - mu
    var = jnp.mean(xc * xc, axis=-1, keepdims=True)
    return xc * lax.rsqrt(var + LN_EPS) * g.astype(jnp.float32) + b.astype(jnp.float32)


def rms_norm(x, g):
    xf = x.astype(jnp.float32)
    return xf * lax.rsqrt(jnp.mean(xf * xf, axis=-1, keepdims=True) + RMS_EPS) * g.astype(jnp.float32)


def apply_rope(t, positions):
    half = t.shape[-1] // 2
    inv_freq = ROPE_THETA ** (-jnp.arange(half, dtype=jnp.float32) / half)
    ang = positions.astype(jnp.float32)[..., None] * inv_freq
    cos = jnp.cos(ang)[:, :, None, :]
    sin = jnp.sin(ang)[:, :, None, :]
    t1 = t[..., :half].astype(jnp.float32)
    t2 = t[..., half:].astype(jnp.float32)
    return jnp.concatenate([t1 * cos - t2 * sin, t1 * sin + t2 * cos], axis=-1).astype(t.dtype)


def gla_mixer(x, w_in, w_a2, b_a, g_out, w_out):
    B, S, _ = x.shape
    nc = S // CHUNK
    h = x @ w_in
    q, k, v, z, a_lr = jnp.split(h, [GLA_HK, 2 * GLA_HK, 2 * GLA_HK + GLA_HV, 2 * GLA_HK + 2 * GLA_HV], axis=-1)
    log_a = jax.nn.log_sigmoid((a_lr @ w_a2 + b_a).astype(jnp.float32)) / GLA_TAU

    def to_chunks(t, d):
        return t.reshape(B, nc, CHUNK, GLA_HEADS, d).transpose(1, 0, 3, 2, 4)

    q_c = to_chunks(q * (GLA_DK ** -0.5), GLA_DK)
    k_c = to_chunks(k, GLA_DK)
    v_c = to_chunks(v, GLA_DV)
    la = to_chunks(log_a, GLA_DK)
    cum = jnp.cumsum(la, axis=3)
    total = cum[:, :, :, -1]
    k_dec = k_c * jnp.exp(total[:, :, :, None, :] - cum)
    chunk_decay = jnp.exp(total)

    def step(state, inp):
        q_i, k_i, v_i, a_i = inp
        state = a_i[..., None] * state + jnp.einsum('bhcd,bhce->bhde', k_i, v_i)
        o_i = jnp.einsum('bhcd,bhde->bhce', q_i, state)
        return state, o_i

    s0 = jnp.zeros((B, GLA_HEADS, GLA_DK, GLA_DV), jnp.float32)
    _, o = lax.scan(step, s0, (q_c, k_dec, v_c, chunk_decay))
    o = o.transpose(1, 0, 3, 2, 4).reshape(B, S, GLA_HEADS, GLA_DV)
    o = rms_norm(o, g_out).reshape(B, S, GLA_HV)
    o = o * jax.nn.silu(z.astype(jnp.float32))
    return o @ w_out


def mla_mixer(x, positions, w_in, g_q, w_uq, g_kv, w_ukv, w_out):
    B, S, _ = x.shape
    h = x @ w_in
    c_q, c_kv, k_rope, z = jnp.split(h, [MLA_Q_RANK, MLA_Q_RANK + MLA_KV_RANK, MLA_Q_RANK + MLA_KV_RANK + MLA_ROPE], axis=-1)
    c_q = rms_norm(c_q, g_q).astype(x.dtype)
    c_kv = rms_norm(c_kv, g_kv).astype(x.dtype)
    q = (c_q @ w_uq).reshape(B, S, MLA_HEADS, MLA_QK)
    kv = (c_kv @ w_ukv).reshape(B, S, MLA_HEADS, MLA_NOPE + MLA_V)
    q_nope, q_rope = q[..., :MLA_NOPE], q[..., MLA_NOPE:]
    k_nope, v = kv[..., :MLA_NOPE], kv[..., MLA_NOPE:]
    q_rope = apply_rope(q_rope, positions)
    k_rope = apply_rope(k_rope[:, :, None, :], positions)
    q_full = jnp.concatenate([q_nope, q_rope], axis=-1) * (MLA_QK ** -0.5)
    k_full = jnp.concatenate([k_nope, jnp.broadcast_to(k_rope, (B, S, MLA_HEADS, MLA_ROPE))], axis=-1)

    nb = S // Q_BLOCK
    q_blocks = q_full.reshape(B, nb, Q_BLOCK, MLA_HEADS, MLA_QK).transpose(1, 0, 2, 3, 4)
    key_chunk = jnp.arange(S) // CHUNK

    def attend(args):
        q_blk, blk = args
        q_chunk = (blk * Q_BLOCK + jnp.arange(Q_BLOCK)) // CHUNK
        s = jnp.einsum('bqhd,bkhd->bhqk', q_blk, k_full).astype(jnp.float32)
        mask = key_chunk[None, :] <= q_chunk[:, None]
        s = jnp.where(mask[None, None], s, -jnp.inf)
        p = jax.nn.softmax(s, axis=-1)
        return jnp.einsum('bhqk,bkhd->bqhd', p.astype(v.dtype), v)

    o = lax.map(attend, (q_blocks, jnp.arange(nb)))
    o = o.transpose(1, 0, 2, 3, 4).reshape(B, S, MLA_HV)
    o = o * jax.nn.silu(z.astype(jnp.float32))
    return o @ w_out


def setup_inputs(seed: int = 0) -> dict:
    key = jax.random.key(seed)
    ks = jax.random.split(key, 16)
    f32 = jnp.float32
    nrm = lambda k, shape, scale: jax.random.normal(k, shape, f32) * scale
    x = jax.random.normal(ks[0], (BATCH, SEQ, D_MODEL), f32)
    offset = jax.random.randint(ks[1], (BATCH, 1), 0, 64, dtype=jnp.int32) * CHUNK
    positions = (jnp.arange(SEQ, dtype=jnp.int32)[None, :] + offset).astype(jnp.int32)
    return {
        "x": x,
        "positions": positions,
        "gla_w_in": nrm(ks[2], (N_GLA, D_MODEL, GLA_IN), D_MODEL ** -0.5),
        "gla_w_a2": nrm(ks[3], (N_GLA, GLA_GATE_RANK, GLA_HK), GLA_GATE_RANK ** -0.5),
        "gla_b_a": nrm(ks[4], (N_GLA, GLA_HK), 0.1),
        "gla_g_out": 1.0 + nrm(ks[5], (N_GLA, GLA_HEADS, GLA_DV), 0.02),
        "gla_w_out": nrm(ks[6], (N_GLA, GLA_HV, D_MODEL), GLA_HV ** -0.5 * DEEPNORM_BETA),
        "mla_w_in": nrm(ks[7], (N_MLA, D_MODEL, MLA_IN), D_MODEL ** -0.5),
        "mla_g_q": 1.0 + nrm(ks[8], (N_MLA, MLA_Q_RANK), 0.02),
        "mla_w_uq": nrm(ks[9], (N_MLA, MLA_Q_RANK, MLA_HEADS * MLA_QK), MLA_Q_RANK ** -0.5),
        "mla_g_kv": 1.0 + nrm(ks[10], (N_MLA, MLA_KV_RANK), 0.02),
        "mla_w_ukv": nrm(ks[11], (N_MLA, MLA_KV_RANK, MLA_HEADS * (MLA_NOPE + MLA_V)), MLA_KV_RANK ** -0.5),
        "mla_w_out": nrm(ks[12], (N_MLA, MLA_HV, D_MODEL), MLA_HV ** -0.5 * DEEPNORM_BETA),
        "ln_g": 1.0 + nrm(ks[13], (DEPTH, D_MODEL), 0.02),
        "ln_b": nrm(ks[14], (DEPTH, D_MODEL), 0.02),
    }


def reference(x, positions, gla_w_in, gla_w_a2, gla_b_a, gla_g_out, gla_w_out,
              mla_w_in, mla_g_q, mla_w_uq, mla_g_kv, mla_w_ukv, mla_w_out, ln_g, ln_b):
    for i in range(DEPTH):
        j = i // N_MIXERS
        if i % N_MIXERS == 0:
            y = gla_mixer(x, gla_w_in[j], gla_w_a2[j], gla_b_a[j], gla_g_out[j], gla_w_out[j])
        else:
            y = mla_mixer(x, positions, mla_w_in[j], mla_g_q[j], mla_w_uq[j], mla_g_kv[j],
                          mla_w_ukv[j], mla_w_out[j])
        x = layer_norm(DEEPNORM_ALPHA * x + y, ln_g[i], ln_b[i]).astype(x.dtype)
    return x
```

```python
import math
from contextlib import ExitStack

import numpy as np
import concourse.bass as bass
import concourse.mybir as mybir
from concourse.bass_utils import run_bass_kernel_spmd

F32 = mybir.dt.float32
BF16 = mybir.dt.bfloat16
I32 = mybir.dt.int32
AF = mybir.ActivationFunctionType
ALU = mybir.AluOpType

T = 2048
D = 2048
NT = 16
KC = 16
ALPHA = float(4 ** 0.25)
LN_EPS = 1e-5
RMS_EPS = 1e-6
TWO_PI = float(2 * np.pi)
QSCALE = float(192 ** -0.5)
MASKNEG = -30000.0
VERBOSE = False


class Tile:
    __slots__ = ("ap", "w", "r", "dsem", "name", "multi")

    def __init__(self, ap, name="", multi=False):
        self.ap = ap
        self.w = {}
        self.r = {}
        self.dsem = None
        self.name = name
        self.multi = multi


class Eng:
    def __init__(self, S, name, sem):
        self.S = S
        self.name = name
        self.sem = sem
        self.count = 0
        self.seen = {}
        self.prog = []

    def wait_for(self, ev):
        if ev is None:
            return
        sem, val = ev
        if sem is self.sem and (self.name == "pe" or not self.S.same_engine_sync):
            return
        if self.seen.get(id(sem), 0) >= val:
            return
        self.prog.append(lambda h: h.wait_ge(sem, val))
        self.seen[id(sem)] = val


class Sched:
    def __init__(self, nc, stack, same_engine_sync=True):
        self.nc = nc
        self.stack = stack
        self.same_engine_sync = same_engine_sync
        self.nsem = 0
        self.sem_pool = []
        self.dma_evs = {}
        self.pe = Eng(self, "pe", self.new_sem("pe"))
        self.act = Eng(self, "act", self.new_sem("act"))
        self.dve = Eng(self, "dve", self.new_sem("dve"))
        self.pool = Eng(self, "pool", self.new_sem("pool"))
        self.sp = Eng(self, "sp", self.new_sem("sp"))
        self.engs = [self.pe, self.act, self.dve, self.pool, self.sp]
        self.uid = 0

    def new_sem(self, name):
        self.nsem += 1
        return self.stack.enter_context(self.nc.semaphore(f"s{self.nsem}_{name}"))

    def sb(self, shape, dt, name=None, side=None):
        self.uid += 1
        if side is None:
            t = self.stack.enter_context(self.nc.sbuf_tensor(f"{name or 't'}_{self.uid}", list(shape), dt))
        else:
            t = self.stack.enter_context(self.nc.sbuf_tensor(f"{name or 't'}_{self.uid}", list(shape), dt, side=side))
        return t

    def tile(self, shape, dt, name=None, multi=False, side=None):
        h = self.sb(shape, dt, name, side)
        return Tile(h[:], name or "t", multi)

    def ps(self, shape, dt, name=None):
        self.uid += 1
        h = self.stack.enter_context(self.nc.psum_tensor(f"{name or 'p'}_{self.uid}", list(shape), dt))
        return Tile(h[:], name or "p")

    def _deps(self, eng, reads, writes):
        for t in reads:
            for v in list(t.w.values()):
                eng.wait_for(v)
        for t in writes:
            if not t.multi:
                for v in list(t.w.values()):
                    eng.wait_for(v)
            for v in list(t.r.values()):
                eng.wait_for(v)

    def _post(self, ev, reads, writes):
        for t in reads:
            t.r[id(ev[0])] = ev
        for t in writes:
            if t.multi:
                t.w[id(ev[0])] = ev
            else:
                t.w = {id(ev[0]): ev}
                t.r = {}

    def op(self, eng, fn, reads=(), writes=(), sig=True):
        if self.dead:
            return
        self._deps(eng, reads, writes)
        if sig:
            eng.count += 1
            sem = eng.sem
            eng.prog.append(lambda h: fn().then_inc(sem, 1))
            ev = (eng.sem, eng.count)
        else:
            eng.prog.append(lambda h: fn())
            ev = (eng.sem, eng.count + 1)
        self._post(ev, reads, writes)

    def dma(self, q, out_t, in_t, out_ap=None, in_ap=None, sem_tile=None):
        st = sem_tile
        if self.dead:
            return
        self._deps(q, [in_t], [out_t])
        if st.dsem is None:
            st.dsem = self.sem_pool.pop() if self.sem_pool else self.new_sem("d")
        oa = out_ap if out_ap is not None else out_t.ap
        ia = in_ap if in_ap is not None else in_t.ap
        dsem = st.dsem
        q.prog.append(lambda h: h.dma_start(out=oa, in_=ia).then_inc(dsem, 16))
        cntv = self._semcnt.get(id(dsem), 0) + 16
        self._semcnt[id(dsem)] = cntv
        ev = (dsem, cntv)
        self.dma_evs[id(dsem)] = ev
        self._post(ev, [in_t], [out_t])

    _semcnt = {}

    def release_dma_sems(self, tiles):
        for t in tiles:
            if t.dsem is not None:
                self.sem_pool.append(t.dsem)
                t.dsem = None

    def barrier(self, tag=""):
        if self.dead:
            return
        self.nbar = getattr(self, "nbar", 0) + 1
        if VERBOSE:
            print("barrier", tag, "sbuf_remaining", self.nc.sbuf_bytes_remaining, "sems", self.nsem,
                  "counts", [e.count for e in self.engs], "prog", [len(e.prog) for e in self.engs], flush=True)
        evs = [(e.sem, e.count) for e in self.engs if e.count > 0] + list(self.dma_evs.values())
        for e in self.engs:
            for ev in evs:
                if ev[0] is e.sem:
                    continue
                e.wait_for(ev)
        self.dma_evs = {}
        if self.stop_after is not None and self.nbar >= self.stop_after:
            self.dead = True

    stop_after = None
    dead = False

    def emit(self):
        nc = self.nc
        with nc.Block() as block:
            for eng, deco in ((self.pe, block.tensor), (self.act, block.scalar), (self.dve, block.vector),
                              (self.pool, block.gpsimd), (self.sp, block.sync)):
                def body(h, eng=eng):
                    for f in eng.prog:
                        f(h)
                deco(body)


class _Stop(Exception):
    pass


def build_program(debug=False, stop_after=None):
    nc = bass.Bass("TRN2", target_bir_lowering=False)
    Sched._semcnt = {}

    def din(name, shape, dt=F32):
        return Tile(nc.dram_tensor(name, list(shape), dt, kind="ExternalInput").ap(), name)

    def dscr(name, shape, dt, out=False):
        kind = "ExternalOutput" if (out or debug) else "Internal"
        return Tile(nc.dram_tensor(name, list(shape), dt, kind=kind).ap(), name, multi=True)

    x_d = din("x", [T, D])
    pos_d = din("pos_bc", [128, T], I32)
    g_win = din("g_win", [D, 6144])
    g_walr = din("g_walr", [D, 128])
    g_wa2 = din("g_wa2", [128, 1024])
    g_gout = din("g_gout", [128, 2048])
    g_wout = din("g_wout", [D, D])
    m_wc = din("m_wc", [D, 1024])
    m_wkr = din("m_wkr", [D, 256])
    m_wz = din("m_wz", [D, 2048])
    m_gcol = din("m_gcol", [128, 8])
    m_wuqn = din("m_wuqn", [512, 2048])
    m_wuqr = din("m_wuqr", [512, 1024])
    m_wuqp = din("m_wuqp", [512, 1024])
    m_wuk = din("m_wuk", [512, 2048])
    m_wuv = din("m_wuv", [512, 2048])
    m_wout = din("m_wout", [D, D])
    lng_d = din("lng_bc", [128, 2 * D])
    lnb_d = din("lnb_bc", [128, 2 * D])
    c_ident = din("c_ident", [128, 128])
    c_U = din("c_U", [128, 384])
    c_ind = din("c_ind", [128, 3])
    c_mask = din("c_mask", [128, 4 * 512])
    c_rope = din("c_rope", [128, 2])
    out_d = Tile(nc.dram_tensor("out", [T, D], F32, kind="ExternalOutput").ap(), "out", multi=True)

    k_s = dscr("k_s", [T, 1024], BF16)
    v_s = dscr("v_s", [T, 2048], BF16)
    zs_s = dscr("zs_s", [T, 2048], BF16)
    qT_s = dscr("qT_s", [1024, T], BF16)
    x1_s = dscr("x1_s", [T, D], F32)
    zsT_s = dscr("zsT_s", [2048, T], BF16)
    ogT_s = dscr("ogT_s", [NT, 128, 16, 128], BF16)
    cs_s = dscr("cs_s", [2, 128, T], F32)

    with ExitStack() as top:
        S = Sched(nc, top, same_engine_sync=True)
        S.stop_after = stop_after
        pe, act, dve, pool, sp = S.pe, S.act, S.dve, S.pool, S.sp
        V, A, G, PE_ = nc.vector, nc.scalar, nc.gpsimd, nc.tensor

        PB = [S.ps([128, 512], F32, f"pb{i}") for i in range(7)]
        PT = S.ps([128, 512], BF16, "pt")
        PT2 = Tile(PB[6].ap.bitcast(BF16)[:, 0:512], "pt2")
        PTs = [PT, PT2]

        ident_b = S.tile([128, 128], BF16, "identb")
        ident_f = S.tile([128, 128], F32, "identf")
        ones_b = S.tile([128, 128], BF16, "onesb")
        ones_f = S.tile([128, 128], F32, "onesf")
        S.dma(pool, ident_b, c_ident, sem_tile=ident_b)
        S.dma(pool, ident_f, c_ident, sem_tile=ident_f)
        S.op(dve, lambda: V.memset(ones_b.ap, 1.0), [], [ones_b])
        S.op(dve, lambda: V.memset(ones_f.ap, 1.0), [], [ones_f])

        def mm(out_t, out_ap, pairs, reads):
            n = len(pairs)
            for i, (l, r) in enumerate(pairs):
                S.op(pe, (lambda l=l, r=r, i=i: PE_.matmul(out_ap, l, r, start=(i == 0), stop=(i == n - 1))),
                     reads, [out_t], sig=(i == n - 1))

        def xT_views(xT):
            return [Tile(xT.ap[:, :, i * 128:(i + 1) * 128], f"xT{i}", multi=True) for i in range(NT)]

        def xT_tile(src_d, xT, xTt, xin, i, use_dve=True):
            xb = xin[i % len(xin)]
            S.dma(pool, xb, src_d, in_ap=src_d.ap[i * 128:(i + 1) * 128, :], sem_tile=xb)
            for j in range(4):
                ptile = PTs[j % 2]
                for q in range(4):
                    kc = 4 * j + q
                    S.op(pe, (lambda kc=kc, q=q, xb=xb, ptile=ptile: PE_.transpose(ptile.ap[:, q * 128:(q + 1) * 128], xb.ap[:, kc * 128:(kc + 1) * 128], ident_b.ap)),
                         [xb, ident_b], [ptile], sig=(q == 3))
                dst = xT.ap[:, 4 * j:4 * j + 4, i * 128:(i + 1) * 128]
                src = ptile.ap.rearrange("p (a b) -> p a b", a=4)
                if j % 2 == 0 and use_dve:
                    S.op(dve, (lambda dst=dst, src=src: V.tensor_copy(dst, src)), [ptile], [xTt[i]])
                else:
                    S.op(act, (lambda dst=dst, src=src: A.copy(dst, src)), [ptile], [xTt[i]])

        def build_xT(src_d, xT, xTt):
            xin = [S.tile([128, D], BF16, f"xin{j}") for j in range(2)]
            for i in range(NT):
                xT_tile(src_d, xT, xTt, xin, i)
            S.release_dma_sems(xin)

        def layer_norm_tile(r, lng, lnb, outt, stats, mv, rstd):
            for s in range(4):
                S.op(dve, (lambda s=s: V.bn_stats(stats.ap[:, s * 6:(s + 1) * 6], r.ap[:, s * 512:(s + 1) * 512])), [r], [stats])
            S.op(dve, lambda: V.bn_aggr(mv.ap, stats.ap), [stats], [mv])
            S.op(act, lambda: A.activation(rstd.ap, mv.ap[:, 1:2], AF.Ln, bias=LN_EPS_AP.ap), [mv, LN_EPS_AP], [rstd])
            S.op(act, lambda: A.activation(rstd.ap, rstd.ap, AF.Exp, scale=-0.5), [rstd], [rstd])
            S.op(dve, lambda: V.scalar_tensor_tensor(outt.ap, r.ap, mv.ap[:, 0:1], lng.ap, ALU.subtract, ALU.mult), [r, mv, lng], [outt])
            S.op(dve, lambda: V.scalar_tensor_tensor(outt.ap, outt.ap, rstd.ap, lnb.ap, ALU.mult, ALU.add), [outt, rstd, lnb], [outt])

        LN_EPS_AP = S.tile([128, 1], F32, "lneps")
        S.op(dve, lambda: V.memset(LN_EPS_AP.ap, LN_EPS), [], [LN_EPS_AP])
        RMS_EPS_AP = S.tile([128, 1], F32, "rmseps")
        S.op(dve, lambda: V.memset(RMS_EPS_AP.ap, RMS_EPS), [], [RMS_EPS_AP])

        def alloc_wout():
            return [S.tile([128, KC, 512], BF16, f"wout{n}") for n in range(4)]

        def load_wout(w_d, ws=None):
            ws = ws if ws is not None else alloc_wout()
            for n in range(4):
                src = w_d.ap[:, n * 512:(n + 1) * 512].rearrange("(c p) n -> p c n", p=128)
                for hh in range(2):
                    S.dma(pool, ws[n], w_d, out_ap=ws[n].ap[:, hh * 8:(hh + 1) * 8, :], in_ap=src[:, hh * 8:(hh + 1) * 8, :], sem_tile=ws[n])
            return ws

        def outproj_phase(w_src, resid_d, layer, dst_d, xT_out=None):
            with ExitStack() as ph:
                S.stack = ph
                wout = w_src if isinstance(w_src, list) else load_wout(w_src)
                NOB = 1 if xT_out is not None else 2
                obf = S.tile([128, D // 2], BF16, "obf") if xT_out is not None else None

                def xT_part(i):
                    ob_ = o_t[i % NOB]
                    for hf in range(2):
                        S.op(act, (lambda hf=hf: A.copy(obf.ap, ob_.ap[:, hf * 1024:(hf + 1) * 1024])), [ob_], [obf])
                        for jj in range(2):
                            j = 2 * hf + jj
                            ptile = PTs[j % 2]
                            for q in range(4):
                                kcl = 4 * jj + q
                                S.op(pe, (lambda kcl=kcl, q=q, ptile=ptile: PE_.transpose(ptile.ap[:, q * 128:(q + 1) * 128], obf.ap[:, kcl * 128:(kcl + 1) * 128], ident_b.ap)),
                                     [obf, ident_b], [ptile], sig=(q == 3))
                            dst = xT_out.ap[:, 4 * j:4 * j + 4, i * 128:(i + 1) * 128]
                            src = ptile.ap.rearrange("p (a b) -> p a b", a=4)
                            S.op(act, (lambda dst=dst, src=src: A.copy(dst, src)), [ptile], [xT_out])
                lng = S.tile([128, D], F32, "lng")
                lnb = S.tile([128, D], F32, "lnb")
                S.dma(pool, lng, lng_d, in_ap=lng_d.ap[:, layer * D:(layer + 1) * D], sem_tile=lng)
                S.dma(pool, lnb, lnb_d, in_ap=lnb_d.ap[:, layer * D:(layer + 1) * D], sem_tile=lnb)
                ogt = [S.tile([128, KC, 128], BF16, f"ogt{j}") for j in range(2)]
                x_t = [S.tile([128, D], F32, f"mx{j}") for j in range(2)]
                o_t = [S.tile([128, D], F32, f"mo{j}") for j in range(NOB)]
                r_t = S.tile([128, D], F32, "mr")
                stats = S.tile([128, 24], F32, "mstats")
                mv = S.tile([128, 2], F32, "mmv")
                rstd = S.tile([128, 1], F32, "mrstd")

                def loads3(i):
                    b = i % 2
                    S.dma(pool, ogt[b], ogT_s, in_ap=ogT_s.ap[i], sem_tile=ogt[b])
                    S.dma(pool, x_t[b], resid_d, in_ap=resid_d.ap[i * 128:(i + 1) * 128, :], sem_tile=x_t[b])

                loads3(0)
                cnt = 0
                for i in range(NT):
                    if i + 1 < NT:
                        loads3(i + 1)
                    b = i % 2
                    for n in range(4):
                        pb = PB[cnt % 4]
                        cnt += 1
                        mm(pb, pb.ap, [(ogt[b].ap[:, kc, :], wout[n].ap[:, kc, :]) for kc in range(KC)], [ogt[b], wout[n]])
                        S.op(dve, (lambda pb=pb, n=n, b=b: V.scalar_tensor_tensor(r_t.ap[:, n * 512:(n + 1) * 512], x_t[b].ap[:, n * 512:(n + 1) * 512], ALPHA, pb.ap, ALU.mult, ALU.add)),
                             [x_t[b], pb], [r_t])
                    if xT_out is not None and i >= 1:
                        xT_part(i - 1)
                    layer_norm_tile(r_t, lng, lnb, o_t[i % NOB], stats, mv, rstd)
                    S.dma(sp, dst_d, o_t[i % NOB], out_ap=dst_d.ap[i * 128:(i + 1) * 128, :], sem_tile=o_t[i % NOB])
                if xT_out is not None:
                    xT_part(NT - 1)
                S.barrier()
                S.release_dma_sems((wout if not isinstance(w_src, list) else []) + [lng, lnb] + ogt + x_t + o_t)

        xstack = ExitStack()
        with ExitStack() as l0:
            S.stack = l0
            a_lrT = S.tile([128, T], BF16, "alrT")
            S.op(dve, lambda: V.memset(a_lrT.ap, 0.0), [], [a_lrT])
            S.op(dve, lambda: V.memset(a_lrT.ap[32:33, :], 1.0), [], [a_lrT])
            with ExitStack() as p1:
                S.stack = p1
                xT = S.tile([128, KC, T], BF16, "xT", multi=True)
                xTt = xT_views(xT)
                NXB = 6
                xin = [S.tile([128, D], BF16, f"xin{j}") for j in range(NXB)]
                NSLOT = 3
                wsl = [S.tile([128, KC, 512], BF16, f"wsl{j}") for j in range(NSLOT)]
                stg = [S.tile([128, 512], BF16, f"stg{j}") for j in range(4)]
                walr = S.tile([128, KC, 128], BF16, "walr")

                def load_slab(s):
                    w = wsl[order.index(s) % NSLOT]
                    src = g_win.ap[:, s * 512:(s + 1) * 512].rearrange("(c p) n -> p c n", p=128)
                    for hh in range(2):
                        S.dma(pool, w, g_win, out_ap=w.ap[:, hh * 8:(hh + 1) * 8, :], in_ap=src[:, hh * 8:(hh + 1) * 8, :], sem_tile=w)

                order = [2, 3, 4, 5, 6, 7, 8, 9, 10, 11, 0, 1]
                xT_tile(x_d, xT, xTt, xin, 0, use_dve=False)
                load_slab(order[0])
                for i0 in range(1, NXB - 1):
                    xT_tile(x_d, xT, xTt, xin, i0, use_dve=False)
                load_slab(order[1])
                posi = S.tile([128, T], I32, "posi")
                S.dma(pool, posi, pos_d, sem_tile=posi)
                ropec = S.tile([128, 2], F32, "ropec")
                S.dma(pool, ropec, c_rope, sem_tile=ropec)
                ang = S.tile([128, T], F32, "ang")
                tk = S.tile([128, T], F32, "tk")
                ti = S.tile([128, T], I32, "ti")
                a_s = S.tile([128, T], F32, "a_s")
                a_c = S.tile([128, T], F32, "a_c")
                S.op(dve, lambda: V.tensor_copy(ang.ap, posi.ap), [posi], [ang])
                S.op(dve, lambda: V.tensor_scalar(ang.ap, ang.ap, ropec.ap[:, 0:1], None, ALU.mult), [ang, ropec], [ang])

                def reduce_to(shift, a2):
                    S.op(dve, lambda: V.tensor_scalar(a2.ap, ang.ap, shift, None, ALU.add), [ang], [a2])
                    S.op(dve, lambda: V.tensor_scalar(tk.ap, a2.ap, 1.0 / TWO_PI, None, ALU.mult), [a2], [tk])
                    S.op(dve, lambda: V.tensor_copy(ti.ap, tk.ap), [tk], [ti])
                    S.op(dve, lambda: V.tensor_copy(tk.ap, ti.ap), [ti], [tk])
                    S.op(dve, lambda: V.scalar_tensor_tensor(a2.ap, tk.ap, -TWO_PI, a2.ap, ALU.mult, ALU.add), [tk, a2], [a2])
                    S.op(dve, lambda: V.tensor_scalar(tk.ap, a2.ap, float(np.pi), -TWO_PI, ALU.is_gt, ALU.mult), [a2], [tk])
                    S.op(dve, lambda: V.tensor_tensor(a2.ap, a2.ap, tk.ap, ALU.add), [tk, a2], [a2])
                    S.op(dve, lambda: V.tensor_scalar(tk.ap, a2.ap, float(-np.pi), TWO_PI, ALU.is_lt, ALU.mult), [a2], [tk])
                    S.op(dve, lambda: V.tensor_tensor(a2.ap, a2.ap, tk.ap, ALU.add), [tk, a2], [a2])

                reduce_to(0.0, a_s)
                reduce_to(float(np.pi / 2), a_c)
                S.dma(pool, walr, g_walr, in_ap=g_walr.ap.rearrange("(c p) n -> p c n", p=128), sem_tile=walr)
                cnt = 0
                for si, s in enumerate(order):
                    if si + 2 < 12:
                        load_slab(order[si + 2])
                    w = wsl[si % NSLOT]
                    if s < 2:
                        for m in range(4):
                            for tb in range(4):
                                pb = PB[cnt % 4]
                                st = stg[cnt % 4]
                                cnt += 1
                                mm(pb, pb.ap, [(w.ap[:, kc, m * 128:(m + 1) * 128], xT.ap[:, kc, tb * 512:(tb + 1) * 512]) for kc in range(KC)], [w] + xTt[4 * tb:4 * tb + 4])
                                S.op(act, (lambda pb=pb, st=st: A.mul(st.ap, pb.ap, 1.0 / 16.0)), [pb], [st])
                                f0 = s * 512 + m * 128
                                S.dma(sp, qT_s, st, out_ap=qT_s.ap[f0:f0 + 128, tb * 512:(tb + 1) * 512], sem_tile=st)
                    else:
                        if s < 4:
                            dst, c0 = k_s, (s - 2) * 512
                        elif s < 8:
                            dst, c0 = v_s, (s - 4) * 512
                        else:
                            dst, c0 = zs_s, (s - 8) * 512
                        for i in range(NT):
                            if si == 0 and i + NXB - 1 < NT:
                                xT_tile(x_d, xT, xTt, xin, i + NXB - 1, use_dve=False)
                            pb = PB[cnt % 4]
                            st = stg[cnt % 4]
                            cnt += 1
                            mm(pb, pb.ap, [(xT.ap[:, kc, i * 128:(i + 1) * 128], w.ap[:, kc, :]) for kc in range(KC)], [w, xTt[i]])
                            if s >= 8:
                                S.op(act, (lambda pb=pb, st=st: A.activation(st.ap, pb.ap, AF.Silu)), [pb], [st])
                            else:
                                S.op(act, (lambda pb=pb, st=st: A.copy(st.ap, pb.ap)), [pb], [st])
                            S.dma(sp, dst, st, out_ap=dst.ap[i * 128:(i + 1) * 128, c0:c0 + 512], sem_tile=st)
                for tb in range(4):
                    pb = PB[tb % 4]
                    mm(pb, pb.ap, [(walr.ap[:, kc, :], xT.ap[:, kc, tb * 512:(tb + 1) * 512]) for kc in range(KC)], [walr] + xTt[4 * tb:4 * tb + 4])
                    S.op(act, (lambda pb=pb, tb=tb: A.copy(a_lrT.ap[0:32, tb * 512:(tb + 1) * 512], pb.ap[0:32, :])), [pb], [a_lrT])
                S.op(act, lambda: A.activation(a_s.ap, a_s.ap, AF.Sin), [a_s], [a_s])
                S.op(act, lambda: A.activation(a_c.ap, a_c.ap, AF.Sin), [a_c], [a_c])
                S.op(dve, lambda: V.tensor_scalar(a_s.ap, a_s.ap, ropec.ap[:, 1:2], None, ALU.mult), [a_s, ropec], [a_s])
                S.dma(sp, cs_s, a_s, out_ap=cs_s.ap[0], sem_tile=a_s)
                S.dma(sp, cs_s, a_c, out_ap=cs_s.ap[1], sem_tile=a_c)
                S.barrier()
                S.release_dma_sems(wsl + stg + [walr, posi, ropec, a_s, a_c] + xin)
            S.stack = l0
            wout0 = alloc_wout()
            with ExitStack() as p3:
                S.stack = p3
                wa2 = S.tile([128, 1024], BF16, "wa2")
                S.dma(pool, wa2, g_wa2, sem_tile=wa2)
                Wm = S.tile([128, 128], BF16, "Wm")
                S.dma(pool, Wm, c_U, in_ap=c_U.ap[:, 0:128], sem_tile=Wm)
                m01 = S.tile([128, 128], F32, "m01")
                S.dma(pool, m01, c_U, in_ap=c_U.ap[:, 256:384], sem_tile=m01)
                ind = S.tile([128, 3], BF16, "ind")
                S.dma(pool, ind, c_ind, sem_tile=ind)
                gout = S.tile([128, 2048], F32, "gout")
                S.dma(pool, gout, g_gout, sem_tile=gout)
                state = [S.tile([128, 512], F32, f"st{j}") for j in range(8)]
                state_b = [[S.tile([128, 512], BF16, f"sb{j}_{c}") for c in range(2)] for j in range(8)]
                for j in range(8):
                    S.op(pool, (lambda j=j: G.memset(state[j].ap, 0.0)), [], [state[j]])
                NB = 2
                k_t = [S.tile([128, 1024], BF16, f"k{j}") for j in range(NB)]
                v_t = [S.tile([128, 2048], BF16, f"v{j}") for j in range(3)]
                z_t = [S.tile([128, 2048], BF16, f"z{j}") for j in range(NB)]
                q_t = [S.tile([128, 8, 128], BF16, f"q{j}") for j in range(3)]
                kk = [S.tile([128, 1024], BF16, f"kk{j}") for j in range(NB)]
                kkT = [S.tile([128, 8, 128], BF16, f"kkT{j}") for j in range(NB)]
                cc = [S.tile([128, 8], F32, f"cc{j}") for j in range(NB)]
                lp = S.tile([128, 1024], BF16, "lp")
                lpe = S.tile([128, 1024], F32, "lpe")
                Gm = S.tile([128, 1024], F32, "Gm")
                dec = [S.tile([128, 24], F32, f"dec{j}") for j in range(NB)]
                gz = [S.tile([128, 2048], BF16, f"gz{j}") for j in range(NB)]
                og = [S.tile([128, 2048], BF16, f"og{j}") for j in range(NB)]
                ogT = [S.tile([128, KC, 128], BF16, f"ogT{j}") for j in range(NB)]
                ss = [[S.tile([128, 1], F32, f"ss{j}_{h}") for h in range(4)] for j in range(NB)]
                PTm = [S.tile([128, 128], BF16, f"PTm{j}") for j in range(3)]
                junk = S.tile([128, 512], F32, "junk")
                WB = PB[0:4]
                OB = PB[4:6]
                wcnt = [0]

                def wbank():
                    bk = WB[wcnt[0] % 4]
                    wcnt[0] += 1
                    return bk

                def loads(i):
                    b = i % NB
                    b3 = i % 3
                    sl = slice(i * 128, (i + 1) * 128)
                    S.dma(pool, k_t[b], k_s, in_ap=k_s.ap[sl, :], sem_tile=k_t[b])
                    S.dma(pool, q_t[b3], qT_s, in_ap=qT_s.ap[:, sl].rearrange("(c p) t -> p c t", p=128), sem_tile=q_t[b3])
                    S.dma(pool, v_t[b3], v_s, in_ap=v_s.ap[sl, :], sem_tile=v_t[b3])
                    S.dma(pool, z_t[b], zs_s, in_ap=zs_s.ap[sl, :], sem_tile=z_t[b])

                def pre_la(i):
                    sl = slice(i * 128, (i + 1) * 128)
                    for n in range(2):
                        pb = wbank()
                        mm(pb, pb.ap, [(a_lrT.ap[:, sl], wa2.ap[:, n * 512:(n + 1) * 512])], [a_lrT, wa2])
                        S.op(act, (lambda pb=pb, n=n: A.activation(lpe.ap[:, n * 512:(n + 1) * 512], pb.ap, AF.Exp, scale=-1.0)), [pb], [lpe])
                    S.op(act, lambda: A.activation(lp.ap, lpe.ap, AF.Ln, bias=1.0), [lpe], [lp])

                def pre_D1(i):
                    for n in range(2):
                        pb = wbank()
                        mm(pb, pb.ap, [(Wm.ap, lp.ap[:, n * 512:(n + 1) * 512])], [Wm, lp])
                        S.op(act, (lambda pb=pb, n=n: A.activation(Gm.ap[:, n * 512:(n + 1) * 512], pb.ap, AF.Exp, scale=-1.0 / 16.0)), [pb], [Gm])
                    b = i % NB
                    pb = wbank()
                    for dc in range(8):
                        S.op(pe, (lambda dc=dc, pb=pb: PE_.matmul(pb.ap[:, dc * 3:dc * 3 + 3], lp.ap[:, dc * 128:(dc + 1) * 128], ind.ap, start=True, stop=True)),
                             [lp, ind], [pb], sig=(dc == 7))
                    S.op(act, (lambda pb=pb: A.activation(dec[b].ap, pb.ap[:, 0:24], AF.Exp, scale=-1.0 / 16.0)), [pb], [dec[b]])

                def pre_D2(i):
                    b = i % NB
                    S.op(dve, lambda: V.tensor_tensor(kk[b].ap, k_t[b].ap, Gm.ap, ALU.mult), [k_t[b], Gm], [kk[b]])
                    de = dec[b].ap.rearrange("p (a c) -> p a c", c=3)[:, :, 0]
                    if i == 0:
                        S.op(dve, (lambda de=de: V.tensor_copy(cc[b].ap, de)), [dec[b]], [cc[b]])
                    else:
                        bp = (i - 1) % NB
                        dop = dec[bp].ap.rearrange("p (a c) -> p a c", c=3)[:, :, 1]
                        S.op(dve, (lambda de=de, dop=dop: V.tensor_tensor(cc[b].ap, de, dop, ALU.mult)), [dec[b], dec[bp]], [cc[b]])
                    qv = q_t[i % 3].ap[:, :, 64:128]
                    dv = dec[b].ap.rearrange("p (a c) -> p a c", c=3)[:, :, 1:2].broadcast_to([128, 8, 64])
                    S.op(dve, (lambda: V.tensor_tensor(qv, qv, dv, ALU.mult)), [q_t[i % 3], dec[b]], [q_t[i % 3]])
                    S.op(pool, lambda: G.tensor_tensor(gz[b].ap, z_t[b].ap, gout.ap, ALU.mult), [z_t[b], gout], [gz[b]])

                def pre_D3(i):
                    b = i % NB
                    for j in range(2):
                        ptile = PTs[j % 2]
                        for q in range(4):
                            kc = 4 * j + q
                            S.op(pe, (lambda kc=kc, q=q, ptile=ptile: PE_.transpose(ptile.ap[:, q * 128:(q + 1) * 128], kk[b].ap[:, kc * 128:(kc + 1) * 128], ident_b.ap)),
                                 [kk[b], ident_b], [ptile], sig=(q == 3))
                        dst = kkT[b].ap[:, 4 * j:4 * j + 4, :]
                        src = ptile.ap.rearrange("p (a b) -> p a b", a=4)
                        if j == 0:
                            S.op(act, (lambda dst=dst, src=src: A.copy(dst, src)), [ptile], [kkT[b]])
                        else:
                            S.op(dve, (lambda dst=dst, src=src: V.tensor_copy(dst, src)), [ptile], [kkT[b]])

                pend = []
                OLAG = 1
                hcnt = [0]

                def back(i, h, ptm):
                    b = i % NB
                    par = i % 2
                    ob = OB[(i * 4 + h) % 2]
                    S.op(pe, (lambda: PE_.matmul(ob.ap, q_t[i % 3].ap[:, 2 * h, :], state_b[2 * h][par].ap, start=True, stop=False)),
                         [q_t[i % 3], state_b[2 * h][par]], [ob], sig=False)
                    S.op(pe, (lambda: PE_.matmul(ob.ap, q_t[i % 3].ap[:, 2 * h + 1, :], state_b[2 * h + 1][par].ap, start=False, stop=False)),
                         [q_t[i % 3], state_b[2 * h + 1][par]], [ob], sig=False)
                    S.op(pe, (lambda: PE_.matmul(ob.ap, ptm.ap, v_t[i % 3].ap[:, h * 512:(h + 1) * 512], start=False, stop=True)),
                         [ptm, v_t[i % 3]], [ob], sig=True)
                    sst = ss[b][h]
                    S.op(act, (lambda: A.activation(junk.ap, ob.ap, AF.Square, accum_out=sst.ap)), [ob], [junk, sst])
                    S.op(act, (lambda: A.activation(sst.ap, sst.ap, AF.Ln, bias=RMS_EPS_AP.ap, scale=1.0 / 512.0)), [sst, RMS_EPS_AP], [sst])
                    S.op(act, (lambda: A.activation(sst.ap, sst.ap, AF.Exp, scale=-0.5)), [sst], [sst])
                    S.op(dve, (lambda: V.scalar_tensor_tensor(og[b].ap[:, h * 512:(h + 1) * 512], ob.ap, sst.ap, gz[b].ap[:, h * 512:(h + 1) * 512], ALU.mult, ALU.mult)),
                         [ob, sst, gz[b]], [og[b]])

                def chain_h(i, h):
                    b = i % NB
                    par = i % 2
                    for j in range(2):
                        sj = 2 * h + j
                        S.op(act, (lambda sj=sj: A.activation(state_b[sj][par].ap, state[sj].ap, AF.Copy, scale=cc[b].ap[:, sj:sj + 1])),
                             [state[sj], cc[b]], [state_b[sj][par]])
                    pbp = wbank()
                    mm(pbp, pbp.ap[:, 0:128], [(kkT[b].ap[:, 2 * h + j, :], q_t[i % 3].ap[:, 2 * h + j, :]) for j in range(2)], [kkT[b], q_t[i % 3]])
                    ptm = PTm[hcnt[0] % 3]
                    hcnt[0] += 1
                    S.op(dve, (lambda pbp=pbp, ptm=ptm: V.tensor_tensor(ptm.ap, pbp.ap[:, 0:128], m01.ap, ALU.mult)), [pbp, m01], [ptm])
                    for j in range(2):
                        sj = 2 * h + j
                        pb = wbank()
                        mm(pb, pb.ap, [(kk[b].ap[:, sj * 128:(sj + 1) * 128], v_t[i % 3].ap[:, h * 512:(h + 1) * 512])], [kk[b], v_t[i % 3]])
                        S.op(dve, (lambda pb=pb, sj=sj: V.scalar_tensor_tensor(state[sj].ap, state[sj].ap, cc[b].ap[:, sj:sj + 1], pb.ap, ALU.mult, ALU.add)),
                             [state[sj], cc[b], pb], [state[sj]])
                    pend.append((i, h, ptm))
                    if len(pend) > OLAG:
                        back(*pend.pop(0))

                def otrans(i):
                    b = i % NB
                    for j in range(4):
                        ptile = PTs[j % 2]
                        for q in range(4):
                            kc = 4 * j + q
                            S.op(pe, (lambda kc=kc, q=q, ptile=ptile: PE_.transpose(ptile.ap[:, q * 128:(q + 1) * 128], og[b].ap[:, kc * 128:(kc + 1) * 128], ident_b.ap)),
                                 [og[b], ident_b], [ptile], sig=(q == 3))
                        dst = ogT[b].ap[:, 4 * j:4 * j + 4, :]
                        src = ptile.ap.rearrange("p (a b) -> p a b", a=4)
                        S.op(dve, (lambda dst=dst, src=src: V.tensor_copy(dst, src)), [ptile], [ogT[b]])
                    S.dma(sp, ogT_s, ogT[b], out_ap=ogT_s.ap[i], sem_tile=ogT[b])

                loads(0)
                loads(1)
                load_wout(g_wout, wout0)
                pre_la(0)
                pre_D1(0)
                pre_D2(0)
                pre_D3(0)
                for i in range(NT):
                    if i + 1 < NT:
                        pre_la(i + 1)
                    chain_h(i, 0)
                    if i + 2 < NT:
                        loads(i + 2)
                    if i + 1 < NT:
                        pre_D1(i + 1)
                    chain_h(i, 1)
                    if i + 1 < NT:
                        pre_D2(i + 1)
                    chain_h(i, 2)
                    if i + 1 < NT:
                        pre_D3(i + 1)
                    chain_h(i, 3)
                    if i >= 1:
                        otrans(i - 1)
                while pend:
                    back(*pend.pop(0))
                otrans(NT - 1)
                S.barrier()
                S.release_dma_sems([wa2, Wm, m01, ind, gout] + k_t + v_t + z_t + q_t + ogT)
            S.stack = xstack
            rpad = S.tile([128, 8192], BF16, "rpad", side="right")
            x1T = S.tile([128, KC, T], BF16, "x1T", multi=True, side="right")
            S.stack = l0
            outproj_phase(wout0, x_d, 0, x1_s, xT_out=x1T)
            S.release_dma_sems(wout0)
            S.stack = l0
            S.barrier()
        S.stack = top

        with ExitStack() as l1:
            S.stack = l1
            cgT = S.tile([128, 8, T], BF16, "cgT")
            rq_bc = S.tile([128, T], F32, "rq")
            rkv_bc = S.tile([128, T], F32, "rkv")
            rkv_col = S.tile([128, NT], F32, "rkvc")
            cosT = S.tile([128, T], F32, "cosT")
            sinS = S.tile([128, T], F32, "sinS")
            kr_lo = S.tile([128, T], BF16, "krlo")
            kr_hi = S.tile([128, T], BF16, "krhi")
            gcol = S.tile([128, 8], F32, "gcol")
            S.op(pool, lambda: G.memset(kr_lo.ap, 0.0), [], [kr_lo])
            S.op(pool, lambda: G.memset(kr_hi.ap, 0.0), [], [kr_hi])
            with ExitStack() as m1:
                S.stack = m1
                xT = x1T
                wsl = [S.tile([128, KC, 512], BF16, f"mw{j}") for j in range(2)]
                stg = [S.tile([128, 512], BF16, f"mstg{j}") for j in range(4)]
                sq = [S.tile([128, 512], BF16, f"sq{j}") for j in range(2)]
                wkr = S.tile([128, KC, 256], BF16, "wkr")
                S.dma(pool, wkr, m_wkr, in_ap=m_wkr.ap.rearrange("(c p) n -> p c n", p=128), sem_tile=wkr)
                t1 = S.tile([128, 512], F32, "t1")
                t2 = S.tile([128, 512], F32, "t2")
                krf = S.tile([128, 512], BF16, "krf")
                slabs = [(m_wc, 0), (m_wc, 512), (m_wz, 0), (m_wz, 512), (m_wz, 1024), (m_wz, 1536)]

                def load_slab(s):
                    w = wsl[s % 2]
                    srcd, c0 = slabs[s]
                    src = srcd.ap[:, c0:c0 + 512].rearrange("(c p) n -> p c n", p=128)
                    for hh in range(2):
                        S.dma(pool, w, srcd, out_ap=w.ap[:, hh * 8:(hh + 1) * 8, :], in_ap=src[:, hh * 8:(hh + 1) * 8, :], sem_tile=w)

                load_slab(0)
                S.dma(pool, sinS, cs_s, in_ap=cs_s.ap[0], sem_tile=sinS)
                S.dma(pool, cosT, cs_s, in_ap=cs_s.ap[1], sem_tile=cosT)
                S.dma(pool, gcol, m_gcol, sem_tile=gcol)
                for tb in range(4):
                    ts = slice(tb * 512, (tb + 1) * 512)
                    pa, pbb = PB[0], PB[1]
                    mm(pa, pa.ap, [(wkr.ap[:, kc, 0:128], xT.ap[:, kc, ts]) for kc in range(KC)], [wkr, xT])
                    mm(pbb, pbb.ap, [(wkr.ap[:, kc, 128:256], xT.ap[:, kc, ts]) for kc in range(KC)], [wkr, xT])
                    S.op(dve, (lambda ts=ts, t1=t1: V.tensor_tensor(t1.ap, PB[0].ap, cosT.ap[:, ts], ALU.mult)), [PB[0], cosT], [t1])
                    S.op(dve, (lambda ts=ts, t2=t2: V.tensor_tensor(t2.ap, PB[1].ap, sinS.ap[:, ts], ALU.mult)), [PB[1], sinS], [t2])
                    S.op(pool, (lambda ts=ts, t1=t1, t2=t2: G.tensor_tensor(kr_lo.ap[0:64, ts], t1.ap[0:64, :], t2.ap[0:64, :], ALU.add)), [t1, t2], [kr_lo])
                    S.op(pool, (lambda ts=ts, t1=t1, t2=t2: G.tensor_tensor(kr_hi.ap[64:128, ts], t1.ap[64:128, :], t2.ap[64:128, :], ALU.add)), [t1, t2], [kr_hi])
                cnt = 0
                for s in range(6):
                    if s + 1 < 6:
                        load_slab(s + 1)
                    w = wsl[s % 2]
                    if s < 2:
                        rdst = rq_bc if s == 0 else rkv_bc

                        def ssq_flush(p):
                            ssb, sqt, m, ts, rdst_ = p
                            S.op(pe, (lambda: PE_.matmul(ssb.ap, ones_b.ap, sqt.ap, start=(m == 0), stop=(m == 3))),
                                 [ones_b, sqt], [ssb], sig=(m == 3))
                            if m == 3:
                                S.op(act, (lambda: A.activation(rdst_.ap[:, ts], ssb.ap, AF.Ln, bias=RMS_EPS_AP.ap, scale=1.0 / 512.0)), [ssb, RMS_EPS_AP], [rdst_])
                                S.op(act, (lambda: A.activation(rdst_.ap[:, ts], rdst_.ap[:, ts], AF.Exp, scale=-0.5)), [rdst_], [rdst_])

                        spend = None
                        for tb in range(4):
                            ts = slice(tb * 512, (tb + 1) * 512)
                            ssb = PB[4 + (tb % 2)]
                            for m in range(4):
                                pb = PB[cnt % 4]
                                sqt = sq[cnt % 2]
                                cnt += 1
                                mm(pb, pb.ap, [(w.ap[:, kc, m * 128:(m + 1) * 128], xT.ap[:, kc, ts]) for kc in range(KC)], [w, xT])
                                gi = s * 4 + m
                                S.op(act, (lambda pb=pb, gi=gi, ts=ts: A.activation(cgT.ap[:, gi, ts], pb.ap, AF.Copy, scale=gcol.ap[:, gi:gi + 1])), [pb, gcol], [cgT])
                                S.op(act, (lambda pb=pb, sqt=sqt: A.activation(sqt.ap, pb.ap, AF.Square)), [pb], [sqt])
                                if spend is not None:
                                    ssq_flush(spend)
                                spend = (ssb, sqt, m, ts, rdst)
                        ssq_flush(spend)
                    else:
                        for m in range(4):
                            for tb in range(4):
                                ts = slice(tb * 512, (tb + 1) * 512)
                                pb = PB[cnt % 4]
                                st = stg[cnt % 4]
                                cnt += 1
                                mm(pb, pb.ap, [(w.ap[:, kc, m * 128:(m + 1) * 128], xT.ap[:, kc, ts]) for kc in range(KC)], [w, xT])
                                S.op(act, (lambda pb=pb, st=st: A.activation(st.ap, pb.ap, AF.Silu)), [pb], [st])
                                f0 = (s - 2) * 512 + m * 128
                                S.dma(sp, zsT_s, st, out_ap=zsT_s.ap[f0:f0 + 128, ts], sem_tile=st)
                pb = PB[6]
                for i in range(NT):
                    S.op(pe, (lambda i=i, pb=pb: PE_.matmul(pb.ap[:, i:i + 1], rkv_bc.ap[:, i * 128:(i + 1) * 128], ident_f.ap[:, 0:1], start=True, stop=True)),
                         [rkv_bc, ident_f], [pb], sig=(i == NT - 1))
                S.op(dve, (lambda pb=pb: V.tensor_copy(rkv_col.ap, pb.ap[:, 0:NT])), [pb], [rkv_col])
                S.barrier()
                S.release_dma_sems(wsl + stg + [wkr])
            xstack.close()
            S.stack = l1
            with ExitStack() as m2:
                S.stack = m2
                maskb = S.tile([128, 4 * 512], BF16, "mask")
                S.dma(pool, maskb, c_mask, sem_tile=maskb)
                NG = 2
                wqn = [S.tile([128, 4, 512], BF16, f"wqn{j}") for j in range(NG)]
                wqr = [S.tile([128, 4, 256], BF16, f"wqr{j}") for j in range(NG)]
                wqp = [S.tile([128, 4, 256], BF16, f"wqp{j}") for j in range(NG)]
                wuk = [S.tile([128, 4, 512], BF16, f"wuk{j}") for j in range(NG)]
                wuv = [S.tile([128, 4, 512], BF16, f"wuv{j}") for j in range(NG)]
                qn = [S.tile([128, T], BF16, f"qn{j}") for j in range(4)]
                kn = [S.tile([128, T], BF16, f"kn{j}") for j in range(4)]
                qr = [S.tile([128, T], BF16, f"qr{j}") for j in range(2)]
                vg = S.tile([128, NT, 512], BF16, "vg")
                pT = [S.tile([128, 512], BF16, f"pT{j}") for j in range(4)]
                t1 = [S.tile([128, 512], F32, f"t1_{j}") for j in range(2)]
                t2 = [S.tile([128, 512], F32, f"t2_{j}") for j in range(2)]
                rs = [S.tile([128, 512], F32, f"rs{j}") for j in range(2)]
                of = [S.tile([128, 512], F32, f"of{j}") for j in range(2)]
                zt = [S.tile([128, 512], BF16, f"zt{j}") for j in range(2)]
                ost = [S.tile([128, 512], BF16, f"ost{j}") for j in range(2)]

                def load_group(g):
                    b = g % NG
                    r4 = lambda d: d.ap.rearrange("(c p) n -> p c n", p=128)
                    S.dma(pool, wqn[b], m_wuqn, in_ap=r4(m_wuqn)[:, :, g * 512:(g + 1) * 512], sem_tile=wqn[b])
                    S.dma(pool, wqr[b], m_wuqr, in_ap=r4(m_wuqr)[:, :, g * 256:(g + 1) * 256], sem_tile=wqr[b])
                    S.dma(pool, wqp[b], m_wuqp, in_ap=r4(m_wuqp)[:, :, g * 256:(g + 1) * 256], sem_tile=wqp[b])
                    S.dma(pool, wuk[b], m_wuk, in_ap=r4(m_wuk)[:, :, g * 512:(g + 1) * 512], sem_tile=wuk[b])
                    S.dma(pool, wuv[b], m_wuv, in_ap=r4(m_wuv)[:, :, g * 512:(g + 1) * 512], sem_tile=wuv[b])

                load_group(0)
                cnt = 0
                zc = 0
                for g in range(4):
                    if g + 1 < 4:
                        load_group(g + 1)
                    b = g % NG
                    for hh in range(4):
                        for tb in range(4):
                            ts = slice(tb * 512, (tb + 1) * 512)
                            pb = PB[cnt % 7]
                            cnt += 1
                            mm(pb, pb.ap, [(wqn[b].ap[:, kc, hh * 128:(hh + 1) * 128], cgT.ap[:, kc, ts]) for kc in range(4)], [wqn[b], cgT])
                            S.op(dve, (lambda pb=pb, hh=hh, ts=ts: V.scalar_tensor_tensor(qn[hh].ap[:, ts], pb.ap, QSCALE, rq_bc.ap[:, ts], ALU.mult, ALU.mult)), [pb, rq_bc], [qn[hh]])
                            pb = PB[cnt % 7]
                            cnt += 1
                            mm(pb, pb.ap, [(wuk[b].ap[:, kc, hh * 128:(hh + 1) * 128], cgT.ap[:, 4 + kc, ts]) for kc in range(4)], [wuk[b], cgT])
                            S.op(dve, (lambda pb=pb, hh=hh, ts=ts: V.tensor_tensor(kn[hh].ap[:, ts], pb.ap, rkv_bc.ap[:, ts], ALU.mult)), [pb, rkv_bc], [kn[hh]])
                    for i in range(NT):
                        pb = PB[cnt % 7]
                        cnt += 1
                        mm(pb, pb.ap, [(cgT.ap[:, 4 + kc, i * 128:(i + 1) * 128], wuv[b].ap[:, kc, :]) for kc in range(4)], [wuv[b], cgT])
                        S.op(act, (lambda pb=pb, i=i: A.activation(vg.ap[:, i, :], pb.ap, AF.Copy, scale=rkv_col.ap[:, i:i + 1])), [pb, rkv_col], [vg])
                    for pp in range(2):
                        for tb in range(4):
                            ts = slice(tb * 512, (tb + 1) * 512)
                            pa = PB[cnt % 7]
                            pbb = PB[(cnt + 1) % 7]
                            cnt += 2
                            ta, tb_ = t1[(pp * 4 + tb) % 2], t2[(pp * 4 + tb) % 2]
                            mm(pa, pa.ap, [(wqr[b].ap[:, kc, pp * 128:(pp + 1) * 128], cgT.ap[:, kc, ts]) for kc in range(4)], [wqr[b], cgT])
                            mm(pbb, pbb.ap, [(wqp[b].ap[:, kc, pp * 128:(pp + 1) * 128], cgT.ap[:, kc, ts]) for kc in range(4)], [wqp[b], cgT])
                            S.op(dve, (lambda ts=ts, pa=pa, ta=ta: V.tensor_tensor(ta.ap, pa.ap, cosT.ap[:, ts], ALU.mult)), [pa, cosT], [ta])
                            S.op(dve, (lambda ts=ts, pbb=pbb, tb_=tb_: V.tensor_tensor(tb_.ap, pbb.ap, sinS.ap[:, ts], ALU.mult)), [pbb, sinS], [tb_])
                            S.op(pool, (lambda ta=ta, tb_=tb_: G.tensor_tensor(ta.ap, ta.ap, tb_.ap, ALU.add)), [ta, tb_], [ta])
                            S.op(dve, (lambda pp=pp, ts=ts, ta=ta: V.scalar_tensor_tensor(qr[pp].ap[:, ts], ta.ap, QSCALE, rq_bc.ap[:, ts], ALU.mult, ALU.mult)), [ta, rq_bc], [qr[pp]])
                    cnt = 0
                    blocks = [(hh, qb, kb) for hh in range(4) for qb in range(4) for kb in range(4 * qb + 4)]
                    LAG = 2
                    binfo = {}

                    def issue_score(n):
                        hh, qb, kb = blocks[n]
                        head = g * 4 + hh
                        krt = kr_lo if hh % 2 == 0 else kr_hi
                        qrt = qr[hh // 2]
                        qs = slice(qb * 512, (qb + 1) * 512)
                        ks = slice(kb * 128, (kb + 1) * 128)
                        if kb == 0:
                            zi = (hh * 4 + qb) % 2
                            S.dma(pool, zt[zi], zsT_s, in_ap=zsT_s.ap[head * 128:(head + 1) * 128, qs], sem_tile=zt[zi])
                        pb = PB[n % 3]
                        p_t = pT[n % 4]
                        j = kb - 4 * qb
                        q0 = 128 * j if j > 0 else 0
                        nq = 512 - q0
                        qv = slice(qb * 512 + q0, (qb + 1) * 512)
                        pairs = [(kn[hh].ap[:, ks], qn[hh].ap[:, qv]), (krt.ap[:, ks], qrt.ap[:, qv])]
                        rd = [kn[hh], qn[hh], krt, qrt]
                        if j >= 0:
                            pairs.append((ident_b.ap, maskb.ap[:, j * 512 + q0:(j + 1) * 512]))
                            rd = rd + [ident_b, maskb]
                        mm(pb, pb.ap[:, 0:nq], pairs, rd)
                        S.op(act, (lambda pb=pb, p_t=p_t, nq=nq: A.activation(p_t.ap[:, 0:nq], pb.ap[:, 0:nq], AF.Exp)), [pb], [p_t])
                        binfo[n] = (p_t, q0)

                    def issue_pv(n):
                        hh, qb, kb = blocks[n]
                        head = g * 4 + hh
                        nkb = 4 * qb + 4
                        ai = (hh * 4 + qb) % 2
                        ob, sb_ = PB[3 + 2 * ai], PB[4 + 2 * ai]
                        p_t, q0 = binfo.pop(n)
                        nq = 512 - q0
                        S.op(pe, (lambda: PE_.matmul(ob.ap[:, q0:512], vg.ap[:, kb, hh * 128:(hh + 1) * 128], p_t.ap[:, 0:nq], start=(kb == 0), stop=(kb == nkb - 1))),
                             [vg, p_t], [ob], sig=False)
                        S.op(pe, (lambda: PE_.matmul(sb_.ap[:, q0:512], ones_b.ap, p_t.ap[:, 0:nq], start=(kb == 0), stop=(kb == nkb - 1))),
                             [ones_b, p_t], [sb_], sig=True)
                        if kb == nkb - 1:
                            ob.w = dict(sb_.w)
                            rs_, of_, zt_, os_ = rs[ai], of[ai], zt[ai], ost[ai]
                            S.op(dve, (lambda: V.reciprocal(rs_.ap, sb_.ap)), [sb_], [rs_])
                            S.op(dve, (lambda: V.tensor_tensor(of_.ap, ob.ap, rs_.ap, ALU.mult)), [ob, rs_], [of_])
                            S.op(pool, (lambda: G.tensor_tensor(os_.ap, of_.ap, zt_.ap, ALU.mult)), [of_, zt_], [os_])
                            S.dma(sp, ogT_s, os_, out_ap=ogT_s.ap[4 * qb:4 * qb + 4, :, head, :].rearrange("i p t -> p i t"),
                                  in_ap=os_.ap.rearrange("p (i t) -> p i t", i=4), sem_tile=os_)

                    for n in range(len(blocks) + LAG):
                        if n < len(blocks):
                            issue_score(n)
                        if n - LAG >= 0:
                            issue_pv(n - LAG)
                S.barrier()
                S.release_dma_sems([maskb] + wqn + wqr + wqp + wuk + wuv + zt + ost)
            S.stack = l1
            outproj_phase(m_wout, x1_s, 1, out_d)
            S.stack = l1
        S.stack = top
        S.barrier()
        S.emit()
    return nc


def _consts():
    ident = np.eye(128, dtype=np.float32)
    s = np.arange(128)[:, None]
    t = np.arange(128)[None, :]
    W = np.zeros((128, 128), np.float32)
    W[(t < 64) & (s > t) & (s < 64)] = 1.0
    W[(t >= 64) & (s >= 64) & (s <= t)] = -1.0
    U128 = (s > t).astype(np.float32)
    m01 = ((s // 64) <= (t // 64)).astype(np.float32)
    U = np.concatenate([W, U128, m01], axis=1)
    ind = np.zeros((128, 3), np.float32)
    ind[:64, 0] = 1.0
    ind[64:, 1] = 1.0
    ind[:, 2] = 1.0
    mask = np.zeros((128, 4 * 512), np.float32)
    kk = np.arange(128)[:, None]
    qq = np.arange(512)[None, :]
    for j in range(4):
        vis = ((j * 128 + kk) // 64) <= (qq // 64)
        mask[:, j * 512:(j + 1) * 512] = np.where(vis, 0.0, MASKNEG)
    half = 32
    inv = (np.float32(10000.0) ** (-np.arange(half, dtype=np.float32) / np.float32(half))).astype(np.float32)
    rope = np.zeros((128, 2), np.float32)
    p = np.arange(128)
    rope[:, 0] = inv[p % 32]
    rope[:, 1] = np.where((p % 64) < 32, -1.0, 1.0)
    return ident, U, ind, mask, rope


_NC_CACHE = {}


def _prep_shared(gla_w_in, gla_w_a2, gla_b_a, gla_g_out, gla_w_out, mla_w_in, mla_g_q, mla_w_uq, mla_g_kv,
                 mla_w_ukv, mla_w_out, ln_g, ln_b):
    f = lambda a: np.ascontiguousarray(np.asarray(a, dtype=np.float32))
    gw = f(gla_w_in[0])
    g_win = f(gw[:, :6144])
    g_walr = np.zeros((D, 128), np.float32)
    g_walr[:, :16] = gw[:, 6144:6160]
    g_wa2 = np.zeros((128, 1024), np.float32)
    g_wa2[:16] = f(gla_w_a2[0])
    g_wa2[32] = f(gla_b_a[0])
    g_gout = f(np.broadcast_to(f(gla_g_out[0]).reshape(1, 2048), (128, 2048)))
    mw = f(mla_w_in[0])
    m_wc = f(mw[:, :1024])
    kr = mw[:, 1024:1088]
    krp = np.concatenate([kr[:, 32:], kr[:, :32]], axis=1)
    m_wkr = f(np.concatenate([kr, kr, krp, krp], axis=1))
    m_wz = f(mw[:, 1088:])
    m_gcol = f(np.concatenate([f(mla_g_q[0]).reshape(4, 128).T, f(mla_g_kv[0]).reshape(4, 128).T], axis=1))
    uq = f(mla_w_uq[0]).reshape(512, 16, 192)
    m_wuqn = f(uq[:, :, :128].reshape(512, 2048))
    rope = uq[:, :, 128:]
    m_wuqr = f(rope.reshape(512, 1024))
    m_wuqp = f(np.concatenate([rope[:, :, 32:], rope[:, :, :32]], axis=2).reshape(512, 1024))
    ukv = f(mla_w_ukv[0]).reshape(512, 16, 256)
    m_wuk = f(ukv[:, :, :128].reshape(512, 2048))
    m_wuv = f(ukv[:, :, 128:].reshape(512, 2048))
    lng = f(np.broadcast_to(f(ln_g).reshape(1, 2 * D), (128, 2 * D)))
    lnb = f(np.broadcast_to(f(ln_b).reshape(1, 2 * D), (128, 2 * D)))
    ident, U, ind, mask, rope_c = _consts()
    return dict(g_win=g_win, g_walr=g_walr, g_wa2=g_wa2, g_gout=g_gout, g_wout=f(gla_w_out[0]),
                m_wc=m_wc, m_wkr=m_wkr, m_wz=m_wz, m_gcol=m_gcol, m_wuqn=m_wuqn, m_wuqr=m_wuqr, m_wuqp=m_wuqp,
                m_wuk=m_wuk, m_wuv=m_wuv, m_wout=f(mla_w_out[0]), lng_bc=lng, lnb_bc=lnb,
                c_ident=ident, c_U=U, c_ind=ind, c_mask=mask, c_rope=rope_c)


def kernel(x, positions, gla_w_in, gla_w_a2, gla_b_a, gla_g_out, gla_w_out, mla_w_in, mla_g_q, mla_w_uq,
           mla_g_kv, mla_w_ukv, mla_w_out, ln_g, ln_b, _debug=False, _cores=8, _stop=None):
    x = np.asarray(x, dtype=np.float32)
    positions = np.asarray(positions).astype(np.int32)
    shared = _prep_shared(gla_w_in, gla_w_a2, gla_b_a, gla_g_out, gla_w_out, mla_w_in, mla_g_q, mla_w_uq,
                          mla_g_kv, mla_w_ukv, mla_w_out, ln_g, ln_b)
    key = (bool(_debug), _stop)
    if key not in _NC_CACHE:
        _NC_CACHE[key] = build_program(debug=_debug, stop_after=_stop)
    nc = _NC_CACHE[key]
    in_maps = []
    for c in range(_cores):
        m = dict(shared)
        m["x"] = np.ascontiguousarray(x[c])
        m["pos_bc"] = np.ascontiguousarray(np.broadcast_to(positions[c].reshape(1, T), (128, T)))
        in_maps.append(m)
    res = run_bass_kernel_spmd(nc, in_maps, core_ids=list(range(_cores)))
    if _debug:
        return res.results
    return np.stack([np.asarray(r["out"], dtype=np.float32) for r in res.results], axis=0)
```

```python
import math
from contextlib import ExitStack

import numpy as np
import concourse.bass as bass
import concourse.mybir as mybir
from concourse.bass_utils import run_bass_kernel_spmd

F32 = mybir.dt.float32
BF16 = mybir.dt.bfloat16
I32 = mybir.dt.int32
AF = mybir.ActivationFunctionType
ALU = mybir.AluOpType

T = 2048
D = 2048
NT = 16
KC = 16
ALPHA = float(4 ** 0.25)
LN_EPS = 1e-5
RMS_EPS = 1e-6
TWO_PI = float(2 * np.pi)
QSCALE = float(192 ** -0.5)
MASKNEG = -30000.0
VERBOSE = False


class Tile:
    __slots__ = ("ap", "w", "r", "dsem", "name", "multi")

    def __init__(self, ap, name="", multi=False):
        self.ap = ap
        self.w = {}
        self.r = {}
        self.dsem = None
        self.name = name
        self.multi = multi


class Eng:
    def __init__(self, S, name, sem):
        self.S = S
        self.name = name
        self.sem = sem
        self.count = 0
        self.seen = {}
        self.prog = []

    def wait_for(self, ev):
        if ev is None:
            return
        sem, val = ev
        if sem is self.sem and (self.name == "pe" or not self.S.same_engine_sync):
            return
        if self.seen.get(id(sem), 0) >= val:
            return
        self.prog.append(lambda h: h.wait_ge(sem, val))
        self.seen[id(sem)] = val


class Sched:
    def __init__(self, nc, stack, same_engine_sync=True):
        self.nc = nc
        self.stack = stack
        self.same_engine_sync = same_engine_sync
        self.nsem = 0
        self.sem_pool = []
        self.dma_evs = {}
        self.pe = Eng(self, "pe", self.new_sem("pe"))
        self.act = Eng(self, "act", self.new_sem("act"))
        self.dve = Eng(self, "dve", self.new_sem("dve"))
        self.pool = Eng(self, "pool", self.new_sem("pool"))
        self.sp = Eng(self, "sp", self.new_sem("sp"))
        self.engs = [self.pe, self.act, self.dve, self.pool, self.sp]
        self.uid = 0

    def new_sem(self, name):
        self.nsem += 1
        return self.stack.enter_context(self.nc.semaphore(f"s{self.nsem}_{name}"))

    def sb(self, shape, dt, name=None, side=None):
        self.uid += 1
        if side is None:
            t = self.stack.enter_context(self.nc.sbuf_tensor(f"{name or 't'}_{self.uid}", list(shape), dt))
        else:
            t = self.stack.enter_context(self.nc.sbuf_tensor(f"{name or 't'}_{self.uid}", list(shape), dt, side=side))
        return t

    def tile(self, shape, dt, name=None, multi=False, side=None):
        h = self.sb(shape, dt, name, side)
        return Tile(h[:], name or "t", multi)

    def ps(self, shape, dt, name=None):
        self.uid += 1
        h = self.stack.enter_context(self.nc.psum_tensor(f"{name or 'p'}_{self.uid}", list(shape), dt))
        return Tile(h[:], name or "p")

    def _deps(self, eng, reads, writes):
        for t in reads:
            for v in list(t.w.values()):
                eng.wait_for(v)
        for t in writes:
            if not t.multi:
                for v in list(t.w.values()):
                    eng.wait_for(v)
            for v in list(t.r.values()):
                eng.wait_for(v)

    def _post(self, ev, reads, writes):
        for t in reads:
            t.r[id(ev[0])] = ev
        for t in writes:
            if t.multi:
                t.w[id(ev[0])] = ev
            else:
                t.w = {id(ev[0]): ev}
                t.r = {}

    def op(self, eng, fn, reads=(), writes=(), sig=True):
        if self.dead:
            return
        self._deps(eng, reads, writes)
        if sig:
            eng.count += 1
            sem = eng.sem
            eng.prog.append(lambda h: fn().then_inc(sem, 1))
            ev = (eng.sem, eng.count)
        else:
            eng.prog.append(lambda h: fn())
            ev = (eng.sem, eng.count + 1)
        self._post(ev, reads, writes)

    def dma(self, q, out_t, in_t, out_ap=None, in_ap=None, sem_tile=None):
        st = sem_tile
        if self.dead:
            return
        self._deps(q, [in_t], [out_t])
        if st.dsem is None:
            st.dsem = self.sem_pool.pop() if self.sem_pool else self.new_sem("d")
        oa = out_ap if out_ap is not None else out_t.ap
        ia = in_ap if in_ap is not None else in_t.ap
        dsem = st.dsem
        q.prog.append(lambda h: h.dma_start(out=oa, in_=ia).then_inc(dsem, 16))
        cntv = self._semcnt.get(id(dsem), 0) + 16
        self._semcnt[id(dsem)] = cntv
        ev = (dsem, cntv)
        self.dma_evs[id(dsem)] = ev
        self._post(ev, [in_t], [out_t])

    _semcnt = {}

    def release_dma_sems(self, tiles):
        for t in tiles:
            if t.dsem is not None:
                self.sem_pool.append(t.dsem)
                t.dsem = None

    def barrier(self, tag=""):
        if self.dead:
            return
        self.nbar = getattr(self, "nbar", 0) + 1
        if VERBOSE:
            print("barrier", tag, "sbuf_remaining", self.nc.sbuf_bytes_remaining, "sems", self.nsem,
                  "counts", [e.count for e in self.engs], "prog", [len(e.prog) for e in self.engs], flush=True)
        evs = [(e.sem, e.count) for e in self.engs if e.count > 0] + list(self.dma_evs.values())
        for e in self.engs:
            for ev in evs:
                if ev[0] is e.sem:
                    continue
                e.wait_for(ev)
        self.dma_evs = {}
        if self.stop_after is not None and self.nbar >= self.stop_after:
            self.dead = True

    stop_after = None
    dead = False

    def emit(self):
        nc = self.nc
        with nc.Block() as block:
            for eng, deco in ((self.pe, block.tensor), (self.act, block.scalar), (self.dve, block.vector),
                              (self.pool, block.gpsimd), (self.sp, block.sync)):
                def body(h, eng=eng):
                    for f in eng.prog:
                        f(h)
                deco(body)


class _Stop(Exception):
    pass


def build_program(debug=False, stop_after=None):
    nc = bass.Bass("TRN2", target_bir_lowering=False)
    Sched._semcnt = {}

    def din(name, shape, dt=F32):
        return Tile(nc.dram_tensor(name, list(shape), dt, kind="ExternalInput").ap(), name)

    def dscr(name, shape, dt, out=False):
        kind = "ExternalOutput" if (out or debug) else "Internal"
        return Tile(nc.dram_tensor(name, list(shape), dt, kind=kind).ap(), name, multi=True)

    x_d = din("x", [T, D])
    pos_d = din("pos_bc", [128, T], I32)
    g_win = din("g_win", [D, 6144])
    g_walr = din("g_walr", [D, 128])
    g_wa2 = din("g_wa2", [128, 1024])
    g_gout = din("g_gout", [128, 2048])
    g_wout = din("g_wout", [D, D])
    m_wc = din("m_wc", [D, 1024])
    m_wkr = din("m_wkr", [D, 256])
    m_wz = din("m_wz", [D, 2048])
    m_gcol = din("m_gcol", [128, 8])
    m_wuqn = din("m_wuqn", [512, 2048])
    m_wuqr = din("m_wuqr", [512, 1024])
    m_wuqp = din("m_wuqp", [512, 1024])
    m_wuk = din("m_wuk", [512, 2048])
    m_wuv = din("m_wuv", [512, 2048])
    m_wout = din("m_wout", [D, D])
    lng_d = din("lng_bc", [128, 2 * D])
    lnb_d = din("lnb_bc", [128, 2 * D])
    c_ident = din("c_ident", [128, 128])
    c_U = din("c_U", [128, 384])
    c_ind = din("c_ind", [128, 3])
    c_mask = din("c_mask", [128, 4 * 512])
    c_rope = din("c_rope", [128, 2])
    out_d = Tile(nc.dram_tensor("out", [T, D], F32, kind="ExternalOutput").ap(), "out", multi=True)

    k_s = dscr("k_s", [T, 1024], BF16)
    v_s = dscr("v_s", [T, 2048], BF16)
    zs_s = dscr("zs_s", [T, 2048], BF16)
    qT_s = dscr("qT_s", [1024, T], BF16)
    x1_s = dscr("x1_s", [T, D], F32)
    zsT_s = dscr("zsT_s", [2048, T], BF16)
    ogT_s = dscr("ogT_s", [NT, 128, 16, 128], BF16)
    cs_s = dscr("cs_s", [2, 128, T], F32)

    with ExitStack() as top:
        S = Sched(nc, top, same_engine_sync=True)
        S.stop_after = stop_after
        pe, act, dve, pool, sp = S.pe, S.act, S.dve, S.pool, S.sp
        V, A, G, PE_ = nc.vector, nc.scalar, nc.gpsimd, nc.tensor

        PB = [S.ps([128, 512], F32, f"pb{i}") for i in range(7)]
        PT = S.ps([128, 512], BF16, "pt")
        PT2 = Tile(PB[6].ap.bitcast(BF16)[:, 0:512], "pt2")
        PTs = [PT, PT2]

        ident_b = S.tile([128, 128], BF16, "identb")
        ident_f = S.tile([128, 128], F32, "identf")
        ones_b = S.tile([128, 128], BF16, "onesb")
        ones_f = S.tile([128, 128], F32, "onesf")
        S.dma(pool, ident_b, c_ident, sem_tile=ident_b)
        S.dma(pool, ident_f, c_ident, sem_tile=ident_f)
        S.op(dve, lambda: V.memset(ones_b.ap, 1.0), [], [ones_b])
        S.op(dve, lambda: V.memset(ones_f.ap, 1.0), [], [ones_f])

        def mm(out_t, out_ap, pairs, reads):
            n = len(pairs)
            for i, (l, r) in enumerate(pairs):
                S.op(pe, (lambda l=l, r=r, i=i: PE_.matmul(out_ap, l, r, start=(i == 0), stop=(i == n - 1))),
                     reads, [out_t], sig=(i == n - 1))

        def xT_views(xT):
            return [Tile(xT.ap[:, :, i * 128:(i + 1) * 128], f"xT{i}", multi=True) for i in range(NT)]

        def xT_tile(src_d, xT, xTt, xin, i, use_dve=True):
            xb = xin[i % len(xin)]
            S.dma(pool, xb, src_d, in_ap=src_d.ap[i * 128:(i + 1) * 128, :], sem_tile=xb)
            for j in range(4):
                ptile = PTs[j % 2]
                for q in range(4):
                    kc = 4 * j + q
                    S.op(pe, (lambda kc=kc, q=q, xb=xb, ptile=ptile: PE_.transpose(ptile.ap[:, q * 128:(q + 1) * 128], xb.ap[:, kc * 128:(kc + 1) * 128], ident_b.ap)),
                         [xb, ident_b], [ptile], sig=(q == 3))
                dst = xT.ap[:, 4 * j:4 * j + 4, i * 128:(i + 1) * 128]
                src = ptile.ap.rearrange("p (a b) -> p a b", a=4)
                if j % 2 == 0 and use_dve:
                    S.op(dve, (lambda dst=dst, src=src: V.tensor_copy(dst, src)), [ptile], [xTt[i]])
                else:
                    S.op(act, (lambda dst=dst, src=src: A.copy(dst, src)), [ptile], [xTt[i]])

        def build_xT(src_d, xT, xTt):
            xin = [S.tile([128, D], BF16, f"xin{j}") for j in range(2)]
            for i in range(NT):
                xT_tile(src_d, xT, xTt, xin, i)
            S.release_dma_sems(xin)

        def layer_norm_tile(r, lng, lnb, outt, stats, mv, rstd):
            for s in range(4):
                S.op(dve, (lambda s=s: V.bn_stats(stats.ap[:, s * 6:(s + 1) * 6], r.ap[:, s * 512:(s + 1) * 512])), [r], [stats])
            S.op(dve, lambda: V.bn_aggr(mv.ap, stats.ap), [stats], [mv])
            S.op(act, lambda: A.activation(rstd.ap, mv.ap[:, 1:2], AF.Ln, bias=LN_EPS_AP.ap), [mv, LN_EPS_AP], [rstd])
            S.op(act, lambda: A.activation(rstd.ap, rstd.ap, AF.Exp, scale=-0.5), [rstd], [rstd])
            S.op(dve, lambda: V.scalar_tensor_tensor(outt.ap, r.ap, mv.ap[:, 0:1], lng.ap, ALU.subtract, ALU.mult), [r, mv, lng], [outt])
            S.op(dve, lambda: V.scalar_tensor_tensor(outt.ap, outt.ap, rstd.ap, lnb.ap, ALU.mult, ALU.add), [outt, rstd, lnb], [outt])

        LN_EPS_AP = S.tile([128, 1], F32, "lneps")
        S.op(dve, lambda: V.memset(LN_EPS_AP.ap, LN_EPS), [], [LN_EPS_AP])
        RMS_EPS_AP = S.tile([128, 1], F32, "rmseps")
        S.op(dve, lambda: V.memset(RMS_EPS_AP.ap, RMS_EPS), [], [RMS_EPS_AP])

        def alloc_wout():
            return [S.tile([128, KC, 512], BF16, f"wout{n}") for n in range(4)]

        def load_wout(w_d, ws=None, slabs=(0, 1, 2, 3)):
            ws = ws if ws is not None else alloc_wout()
            for n in slabs:
                src = w_d.ap[:, n * 512:(n + 1) * 512].rearrange("(c p) n -> p c n", p=128)
                for hh in range(2):
                    S.dma(pool, ws[n], w_d, out_ap=ws[n].ap[:, hh * 8:(hh + 1) * 8, :], in_ap=src[:, hh * 8:(hh + 1) * 8, :], sem_tile=ws[n])
            return ws

        def outproj_phase(w_src, resid_d, layer, dst_d, xT_out=None, pre=None):
            with ExitStack() as ph:
                S.stack = ph
                if pre is not None:
                    wout = list(pre) + [S.tile([128, KC, 512], BF16, f"woutl{n}") for n in range(len(pre), 4)]
                    load_wout(w_src, wout, slabs=tuple(range(len(pre), 4)))
                else:
                    wout = w_src if isinstance(w_src, list) else load_wout(w_src)
                NOB = 1 if xT_out is not None else 2
                obf = S.tile([128, D // 2], BF16, "obf") if xT_out is not None else None

                def xT_part(i):
                    ob_ = o_t[i % NOB]
                    for hf in range(2):
                        S.op(act, (lambda hf=hf: A.copy(obf.ap, ob_.ap[:, hf * 1024:(hf + 1) * 1024])), [ob_], [obf])
                        for jj in range(2):
                            j = 2 * hf + jj
                            ptile = PTs[j % 2]
                            for q in range(4):
                                kcl = 4 * jj + q
                                S.op(pe, (lambda kcl=kcl, q=q, ptile=ptile: PE_.transpose(ptile.ap[:, q * 128:(q + 1) * 128], obf.ap[:, kcl * 128:(kcl + 1) * 128], ident_b.ap)),
                                     [obf, ident_b], [ptile], sig=(q == 3))
                            dst = xT_out.ap[:, 4 * j:4 * j + 4, i * 128:(i + 1) * 128]
                            src = ptile.ap.rearrange("p (a b) -> p a b", a=4)
                            S.op(act, (lambda dst=dst, src=src: A.copy(dst, src)), [ptile], [xT_out])
                lng = S.tile([128, D], F32, "lng")
                lnb = S.tile([128, D], F32, "lnb")
                S.dma(pool, lng, lng_d, in_ap=lng_d.ap[:, layer * D:(layer + 1) * D], sem_tile=lng)
                S.dma(pool, lnb, lnb_d, in_ap=lnb_d.ap[:, layer * D:(layer + 1) * D], sem_tile=lnb)
                ogt = [S.tile([128, KC, 128], BF16, f"ogt{j}") for j in range(2)]
                x_t = [S.tile([128, D], F32, f"mx{j}") for j in range(2)]
                o_t = [S.tile([128, D], F32, f"mo{j}") for j in range(NOB)]
                r_t = S.tile([128, D], F32, "mr")
                stats = S.tile([128, 24], F32, "mstats")
                mv = S.tile([128, 2], F32, "mmv")
                rstd = S.tile([128, 1], F32, "mrstd")

                def loads3(i):
                    b = i % 2
                    S.dma(pool, ogt[b], ogT_s, in_ap=ogT_s.ap[i], sem_tile=ogt[b])
                    S.dma(pool, x_t[b], resid_d, in_ap=resid_d.ap[i * 128:(i + 1) * 128, :], sem_tile=x_t[b])

                loads3(0)
                cnt = 0
                for i in range(NT):
                    if i + 1 < NT:
                        loads3(i + 1)
                    b = i % 2
                    for n in range(4):
                        pb = PB[cnt % 4]
                        cnt += 1
                        mm(pb, pb.ap, [(ogt[b].ap[:, kc, :], wout[n].ap[:, kc, :]) for kc in range(KC)], [ogt[b], wout[n]])
                        S.op(dve, (lambda pb=pb, n=n, b=b: V.scalar_tensor_tensor(r_t.ap[:, n * 512:(n + 1) * 512], x_t[b].ap[:, n * 512:(n + 1) * 512], ALPHA, pb.ap, ALU.mult, ALU.add)),
                             [x_t[b], pb], [r_t])
                    if xT_out is not None and i >= 1:
                        xT_part(i - 1)
                    layer_norm_tile(r_t, lng, lnb, o_t[i % NOB], stats, mv, rstd)
                    S.dma(sp, dst_d, o_t[i % NOB], out_ap=dst_d.ap[i * 128:(i + 1) * 128, :], sem_tile=o_t[i % NOB])
                if xT_out is not None:
                    xT_part(NT - 1)
                S.barrier()
                S.release_dma_sems((wout if (pre is not None or not isinstance(w_src, list)) else []) + [lng, lnb] + ogt + x_t + o_t)

        xstack = ExitStack()
        with ExitStack() as l0:
            S.stack = l0
            a_lrT = S.tile([128, T], BF16, "alrT")
            S.op(dve, lambda: V.memset(a_lrT.ap, 0.0), [], [a_lrT])
            S.op(dve, lambda: V.memset(a_lrT.ap[32:33, :], 1.0), [], [a_lrT])
            with ExitStack() as p1:
                S.stack = p1
                xT = S.tile([128, KC, T], BF16, "xT", multi=True)
                xTt = xT_views(xT)
                NXB = 6
                xin = [S.tile([128, D], BF16, f"xin{j}") for j in range(NXB)]
                NSLOT = 3
                wsl = [S.tile([128, KC, 512], BF16, f"wsl{j}") for j in range(NSLOT)]
                stg = [S.tile([128, 512], BF16, f"stg{j}") for j in range(4)]
                walr = S.tile([128, KC, 128], BF16, "walr")

                def load_slab(s):
                    w = wsl[order.index(s) % NSLOT]
                    src = g_win.ap[:, s * 512:(s + 1) * 512].rearrange("(c p) n -> p c n", p=128)
                    for hh in range(2):
                        S.dma(pool, w, g_win, out_ap=w.ap[:, hh * 8:(hh + 1) * 8, :], in_ap=src[:, hh * 8:(hh + 1) * 8, :], sem_tile=w)

                order = [2, 3, 4, 5, 6, 7, 8, 9, 10, 11, 0, 1]
                xT_tile(x_d, xT, xTt, xin, 0, use_dve=False)
                load_slab(order[0])
                for i0 in range(1, NXB - 1):
                    xT_tile(x_d, xT, xTt, xin, i0, use_dve=False)
                load_slab(order[1])
                posi = S.tile([128, T], I32, "posi")
                S.dma(pool, posi, pos_d, sem_tile=posi)
                ropec = S.tile([128, 2], F32, "ropec")
                S.dma(pool, ropec, c_rope, sem_tile=ropec)
                ang = S.tile([128, T], F32, "ang")
                tk = S.tile([128, T], F32, "tk")
                ti = S.tile([128, T], I32, "ti")
                a_s = S.tile([128, T], F32, "a_s")
                a_c = S.tile([128, T], F32, "a_c")
                S.op(dve, lambda: V.tensor_copy(ang.ap, posi.ap), [posi], [ang])
                S.op(dve, lambda: V.tensor_scalar(ang.ap, ang.ap, ropec.ap[:, 0:1], None, ALU.mult), [ang, ropec], [ang])

                def reduce_to(shift, a2):
                    S.op(dve, lambda: V.tensor_scalar(a2.ap, ang.ap, shift, None, ALU.add), [ang], [a2])
                    S.op(dve, lambda: V.tensor_scalar(tk.ap, a2.ap, 1.0 / TWO_PI, None, ALU.mult), [a2], [tk])
                    S.op(dve, lambda: V.tensor_copy(ti.ap, tk.ap), [tk], [ti])
                    S.op(dve, lambda: V.tensor_copy(tk.ap, ti.ap), [ti], [tk])
                    S.op(dve, lambda: V.scalar_tensor_tensor(a2.ap, tk.ap, -TWO_PI, a2.ap, ALU.mult, ALU.add), [tk, a2], [a2])
                    S.op(dve, lambda: V.tensor_scalar(tk.ap, a2.ap, float(np.pi), -TWO_PI, ALU.is_gt, ALU.mult), [a2], [tk])
                    S.op(dve, lambda: V.tensor_tensor(a2.ap, a2.ap, tk.ap, ALU.add), [tk, a2], [a2])
                    S.op(dve, lambda: V.tensor_scalar(tk.ap, a2.ap, float(-np.pi), TWO_PI, ALU.is_lt, ALU.mult), [a2], [tk])
                    S.op(dve, lambda: V.tensor_tensor(a2.ap, a2.ap, tk.ap, ALU.add), [tk, a2], [a2])

                reduce_to(0.0, a_s)
                reduce_to(float(np.pi / 2), a_c)
                S.dma(pool, walr, g_walr, in_ap=g_walr.ap.rearrange("(c p) n -> p c n", p=128), sem_tile=walr)
                cnt = 0
                for si, s in enumerate(order):
                    if si + 2 < 12:
                        load_slab(order[si + 2])
                    w = wsl[si % NSLOT]
                    if s < 2:
                        for m in range(4):
                            for tb in range(4):
                                pb = PB[cnt % 4]
                                st = stg[cnt % 4]
                                cnt += 1
                                mm(pb, pb.ap, [(w.ap[:, kc, m * 128:(m + 1) * 128], xT.ap[:, kc, tb * 512:(tb + 1) * 512]) for kc in range(KC)], [w] + xTt[4 * tb:4 * tb + 4])
                                S.op(act, (lambda pb=pb, st=st: A.mul(st.ap, pb.ap, 1.0 / 16.0)), [pb], [st])
                                f0 = s * 512 + m * 128
                                S.dma(sp, qT_s, st, out_ap=qT_s.ap[f0:f0 + 128, tb * 512:(tb + 1) * 512], sem_tile=st)
                    else:
                        if s < 4:
                            dst, c0 = k_s, (s - 2) * 512
                        elif s < 8:
                            dst, c0 = v_s, (s - 4) * 512
                        else:
                            dst, c0 = zs_s, (s - 8) * 512
                        for i in range(NT):
                            if si == 0 and i + NXB - 1 < NT:
                                xT_tile(x_d, xT, xTt, xin, i + NXB - 1, use_dve=False)
                            pb = PB[cnt % 4]
                            st = stg[cnt % 4]
                            cnt += 1
                            mm(pb, pb.ap, [(xT.ap[:, kc, i * 128:(i + 1) * 128], w.ap[:, kc, :]) for kc in range(KC)], [w, xTt[i]])
                            if s >= 8:
                                S.op(act, (lambda pb=pb, st=st: A.activation(st.ap, pb.ap, AF.Silu)), [pb], [st])
                            else:
                                S.op(act, (lambda pb=pb, st=st: A.copy(st.ap, pb.ap)), [pb], [st])
                            S.dma(sp, dst, st, out_ap=dst.ap[i * 128:(i + 1) * 128, c0:c0 + 512], sem_tile=st)
                for tb in range(4):
                    pb = PB[tb % 4]
                    mm(pb, pb.ap, [(walr.ap[:, kc, :], xT.ap[:, kc, tb * 512:(tb + 1) * 512]) for kc in range(KC)], [walr] + xTt[4 * tb:4 * tb + 4])
                    S.op(act, (lambda pb=pb, tb=tb: A.copy(a_lrT.ap[0:32, tb * 512:(tb + 1) * 512], pb.ap[0:32, :])), [pb], [a_lrT])
                S.op(act, lambda: A.activation(a_s.ap, a_s.ap, AF.Sin), [a_s], [a_s])
                S.op(act, lambda: A.activation(a_c.ap, a_c.ap, AF.Sin), [a_c], [a_c])
                S.op(dve, lambda: V.tensor_scalar(a_s.ap, a_s.ap, ropec.ap[:, 1:2], None, ALU.mult), [a_s, ropec], [a_s])
                S.dma(sp, cs_s, a_s, out_ap=cs_s.ap[0], sem_tile=a_s)
                S.dma(sp, cs_s, a_c, out_ap=cs_s.ap[1], sem_tile=a_c)
                S.barrier()
                S.release_dma_sems(wsl + stg + [walr, posi, ropec, a_s, a_c] + xin)
            S.stack = l0
            wout0 = alloc_wout()
            with ExitStack() as p3:
                S.stack = p3
                wa2 = S.tile([128, 1024], BF16, "wa2")
                S.dma(pool, wa2, g_wa2, sem_tile=wa2)
                Wm = S.tile([128, 128], BF16, "Wm")
                S.dma(pool, Wm, c_U, in_ap=c_U.ap[:, 0:128], sem_tile=Wm)
                m01 = S.tile([128, 128], F32, "m01")
                S.dma(pool, m01, c_U, in_ap=c_U.ap[:, 256:384], sem_tile=m01)
                ind = S.tile([128, 3], BF16, "ind")
                S.dma(pool, ind, c_ind, sem_tile=ind)
                gout = S.tile([128, 2048], F32, "gout")
                S.dma(pool, gout, g_gout, sem_tile=gout)
                state = [S.tile([128, 512], F32, f"st{j}") for j in range(8)]
                state_b = [[S.tile([128, 512], BF16, f"sb{j}_{c}") for c in range(2)] for j in range(8)]
                for j in range(8):
                    S.op(pool, (lambda j=j: G.memset(state[j].ap, 0.0)), [], [state[j]])
                NB = 2
                k_t = [S.tile([128, 1024], BF16, f"k{j}") for j in range(NB)]
                v_t = [S.tile([128, 2048], BF16, f"v{j}") for j in range(3)]
                z_t = [S.tile([128, 2048], BF16, f"z{j}") for j in range(NB)]
                q_t = [S.tile([128, 8, 128], BF16, f"q{j}") for j in range(3)]
                kk = [S.tile([128, 1024], BF16, f"kk{j}") for j in range(NB)]
                kkT = [S.tile([128, 8, 128], BF16, f"kkT{j}") for j in range(NB)]
                cc = [S.tile([128, 8], F32, f"cc{j}") for j in range(NB)]
                lp = S.tile([128, 1024], BF16, "lp")
                lpe = S.tile([128, 1024], F32, "lpe")
                Gm = S.tile([128, 1024], F32, "Gm")
                dec = [S.tile([128, 24], F32, f"dec{j}") for j in range(NB)]
                gz = [S.tile([128, 2048], BF16, f"gz{j}") for j in range(NB)]
                og = [S.tile([128, 2048], BF16, f"og{j}") for j in range(NB)]
                ogT = [S.tile([128, KC, 128], BF16, f"ogT{j}") for j in range(NB)]
                ss = [[S.tile([128, 1], F32, f"ss{j}_{h}") for h in range(4)] for j in range(NB)]
                PTm = [S.tile([128, 128], BF16, f"PTm{j}") for j in range(3)]
                junk = S.tile([128, 512], F32, "junk")
                WB = PB[0:4]
                OB = PB[4:6]
                wcnt = [0]

                def wbank():
                    bk = WB[wcnt[0] % 4]
                    wcnt[0] += 1
                    return bk

                def loads(i):
                    b = i % NB
                    b3 = i % 3
                    sl = slice(i * 128, (i + 1) * 128)
                    S.dma(pool, k_t[b], k_s, in_ap=k_s.ap[sl, :], sem_tile=k_t[b])
                    S.dma(pool, q_t[b3], qT_s, in_ap=qT_s.ap[:, sl].rearrange("(c p) t -> p c t", p=128), sem_tile=q_t[b3])
                    S.dma(pool, v_t[b3], v_s, in_ap=v_s.ap[sl, :], sem_tile=v_t[b3])
                    S.dma(pool, z_t[b], zs_s, in_ap=zs_s.ap[sl, :], sem_tile=z_t[b])

                def pre_la(i):
                    sl = slice(i * 128, (i + 1) * 128)
                    for n in range(2):
                        pb = wbank()
                        mm(pb, pb.ap, [(a_lrT.ap[:, sl], wa2.ap[:, n * 512:(n + 1) * 512])], [a_lrT, wa2])
                        S.op(act, (lambda pb=pb, n=n: A.activation(lpe.ap[:, n * 512:(n + 1) * 512], pb.ap, AF.Exp, scale=-1.0)), [pb], [lpe])
                    S.op(act, lambda: A.activation(lp.ap, lpe.ap, AF.Ln, bias=1.0), [lpe], [lp])

                def pre_D1(i):
                    for n in range(2):
                        pb = wbank()
                        mm(pb, pb.ap, [(Wm.ap, lp.ap[:, n * 512:(n + 1) * 512])], [Wm, lp])
                        S.op(act, (lambda pb=pb, n=n: A.activation(Gm.ap[:, n * 512:(n + 1) * 512], pb.ap, AF.Exp, scale=-1.0 / 16.0)), [pb], [Gm])
                    b = i % NB
                    pb = wbank()
                    for dc in range(8):
                        S.op(pe, (lambda dc=dc, pb=pb: PE_.matmul(pb.ap[:, dc * 3:dc * 3 + 3], lp.ap[:, dc * 128:(dc + 1) * 128], ind.ap, start=True, stop=True)),
                             [lp, ind], [pb], sig=(dc == 7))
                    S.op(act, (lambda pb=pb: A.activation(dec[b].ap, pb.ap[:, 0:24], AF.Exp, scale=-1.0 / 16.0)), [pb], [dec[b]])

                def pre_D2(i):
                    b = i % NB
                    S.op(dve, lambda: V.tensor_tensor(kk[b].ap, k_t[b].ap, Gm.ap, ALU.mult), [k_t[b], Gm], [kk[b]])
                    de = dec[b].ap.rearrange("p (a c) -> p a c", c=3)[:, :, 0]
                    if i == 0:
                        S.op(dve, (lambda de=de: V.tensor_copy(cc[b].ap, de)), [dec[b]], [cc[b]])
                    else:
                        bp = (i - 1) % NB
                        dop = dec[bp].ap.rearrange("p (a c) -> p a c", c=3)[:, :, 1]
                        S.op(dve, (lambda de=de, dop=dop: V.tensor_tensor(cc[b].ap, de, dop, ALU.mult)), [dec[b], dec[bp]], [cc[b]])
                    qv = q_t[i % 3].ap[:, :, 64:128]
                    dv = dec[b].ap.rearrange("p (a c) -> p a c", c=3)[:, :, 1:2].broadcast_to([128, 8, 64])
                    S.op(dve, (lambda: V.tensor_tensor(qv, qv, dv, ALU.mult)), [q_t[i % 3], dec[b]], [q_t[i % 3]])
                    S.op(pool, lambda: G.tensor_tensor(gz[b].ap, z_t[b].ap, gout.ap, ALU.mult), [z_t[b], gout], [gz[b]])

                def pre_D3(i):
                    b = i % NB
                    for j in range(2):
                        ptile = PTs[j % 2]
                        for q in range(4):
                            kc = 4 * j + q
                            S.op(pe, (lambda kc=kc, q=q, ptile=ptile: PE_.transpose(ptile.ap[:, q * 128:(q + 1) * 128], kk[b].ap[:, kc * 128:(kc + 1) * 128], ident_b.ap)),
                                 [kk[b], ident_b], [ptile], sig=(q == 3))
                        dst = kkT[b].ap[:, 4 * j:4 * j + 4, :]
                        src = ptile.ap.rearrange("p (a b) -> p a b", a=4)
                        if j == 0:
                            S.op(act, (lambda dst=dst, src=src: A.copy(dst, src)), [ptile], [kkT[b]])
                        else:
                            S.op(dve, (lambda dst=dst, src=src: V.tensor_copy(dst, src)), [ptile], [kkT[b]])

                pend = []
                OLAG = 1
                hcnt = [0]

                def back(i, h, ptm):
                    b = i % NB
                    par = i % 2
                    ob = OB[(i * 4 + h) % 2]
                    S.op(pe, (lambda: PE_.matmul(ob.ap, q_t[i % 3].ap[:, 2 * h, :], state_b[2 * h][par].ap, start=True, stop=False)),
                         [q_t[i % 3], state_b[2 * h][par]], [ob], sig=False)
                    S.op(pe, (lambda: PE_.matmul(ob.ap, q_t[i % 3].ap[:, 2 * h + 1, :], state_b[2 * h + 1][par].ap, start=False, stop=False)),
                         [q_t[i % 3], state_b[2 * h + 1][par]], [ob], sig=False)
                    S.op(pe, (lambda: PE_.matmul(ob.ap, ptm.ap, v_t[i % 3].ap[:, h * 512:(h + 1) * 512], start=False, stop=True)),
                         [ptm, v_t[i % 3]], [ob], sig=True)
                    sst = ss[b][h]
                    S.op(act, (lambda: A.activation(junk.ap, ob.ap, AF.Square, accum_out=sst.ap)), [ob], [junk, sst])
                    S.op(act, (lambda: A.activation(sst.ap, sst.ap, AF.Ln, bias=RMS_EPS_AP.ap, scale=1.0 / 512.0)), [sst, RMS_EPS_AP], [sst])
                    S.op(act, (lambda: A.activation(sst.ap, sst.ap, AF.Exp, scale=-0.5)), [sst], [sst])
                    S.op(dve, (lambda: V.scalar_tensor_tensor(og[b].ap[:, h * 512:(h + 1) * 512], ob.ap, sst.ap, gz[b].ap[:, h * 512:(h + 1) * 512], ALU.mult, ALU.mult)),
                         [ob, sst, gz[b]], [og[b]])

                def chain_h(i, h):
                    b = i % NB
                    par = i % 2
                    for j in range(2):
                        sj = 2 * h + j
                        S.op(act, (lambda sj=sj: A.activation(state_b[sj][par].ap, state[sj].ap, AF.Copy, scale=cc[b].ap[:, sj:sj + 1])),
                             [state[sj], cc[b]], [state_b[sj][par]])
                    pbp = wbank()
                    mm(pbp, pbp.ap[:, 0:128], [(kkT[b].ap[:, 2 * h + j, :], q_t[i % 3].ap[:, 2 * h + j, :]) for j in range(2)], [kkT[b], q_t[i % 3]])
                    ptm = PTm[hcnt[0] % 3]
                    hcnt[0] += 1
                    S.op(dve, (lambda pbp=pbp, ptm=ptm: V.tensor_tensor(ptm.ap, pbp.ap[:, 0:128], m01.ap, ALU.mult)), [pbp, m01], [ptm])
                    for j in range(2):
                        sj = 2 * h + j
                        pb = wbank()
                        mm(pb, pb.ap, [(kk[b].ap[:, sj * 128:(sj + 1) * 128], v_t[i % 3].ap[:, h * 512:(h + 1) * 512])], [kk[b], v_t[i % 3]])
                        S.op(dve, (lambda pb=pb, sj=sj: V.scalar_tensor_tensor(state[sj].ap, state[sj].ap, cc[b].ap[:, sj:sj + 1], pb.ap, ALU.mult, ALU.add)),
                             [state[sj], cc[b], pb], [state[sj]])
                    pend.append((i, h, ptm))
                    if len(pend) > OLAG:
                        back(*pend.pop(0))

                def otrans(i):
                    b = i % NB
                    for j in range(4):
                        ptile = PTs[j % 2]
                        for q in range(4):
                            kc = 4 * j + q
                            S.op(pe, (lambda kc=kc, q=q, ptile=ptile: PE_.transpose(ptile.ap[:, q * 128:(q + 1) * 128], og[b].ap[:, kc * 128:(kc + 1) * 128], ident_b.ap)),
                                 [og[b], ident_b], [ptile], sig=(q == 3))
                        dst = ogT[b].ap[:, 4 * j:4 * j + 4, :]
                        src = ptile.ap.rearrange("p (a b) -> p a b", a=4)
                        S.op(dve, (lambda dst=dst, src=src: V.tensor_copy(dst, src)), [ptile], [ogT[b]])
                    S.dma(sp, ogT_s, ogT[b], out_ap=ogT_s.ap[i], sem_tile=ogT[b])

                loads(0)
                loads(1)
                load_wout(g_wout, wout0)
                pre_la(0)
                pre_D1(0)
                pre_D2(0)
                pre_D3(0)
                for i in range(NT):
                    if i + 1 < NT:
                        pre_la(i + 1)
                    chain_h(i, 0)
                    if i + 2 < NT:
                        loads(i + 2)
                    if i + 1 < NT:
                        pre_D1(i + 1)
                    chain_h(i, 1)
                    if i + 1 < NT:
                        pre_D2(i + 1)
                    chain_h(i, 2)
                    if i + 1 < NT:
                        pre_D3(i + 1)
                    chain_h(i, 3)
                    if i >= 1:
                        otrans(i - 1)
                while pend:
                    back(*pend.pop(0))
                otrans(NT - 1)
                S.barrier()
                S.release_dma_sems([wa2, Wm, m01, ind, gout] + k_t + v_t + z_t + q_t + ogT)
            S.stack = xstack
            rpad = S.tile([128, 8192], BF16, "rpad", side="right")
            x1T = S.tile([128, KC, T], BF16, "x1T", multi=True, side="right")
            S.stack = l0
            outproj_phase(wout0, x_d, 0, x1_s, xT_out=x1T)
            S.release_dma_sems(wout0)
            S.stack = l0
            S.barrier()
        S.stack = top

        with ExitStack() as l1:
            S.stack = l1
            cgT = S.tile([128, 8, T], BF16, "cgT")
            rq_bc = S.tile([128, T], F32, "rq")
            rkv_bc = S.tile([128, T], F32, "rkv")
            rkv_col = S.tile([128, NT], F32, "rkvc")
            cosT = S.tile([128, T], F32, "cosT")
            sinS = S.tile([128, T], F32, "sinS")
            kr_lo = S.tile([128, T], BF16, "krlo")
            kr_hi = S.tile([128, T], BF16, "krhi")
            gcol = S.tile([128, 8], F32, "gcol")
            S.op(pool, lambda: G.memset(kr_lo.ap, 0.0), [], [kr_lo])
            S.op(pool, lambda: G.memset(kr_hi.ap, 0.0), [], [kr_hi])
            with ExitStack() as m1:
                S.stack = m1
                xT = x1T
                wsl = [S.tile([128, KC, 512], BF16, f"mw{j}") for j in range(2)]
                stg = [S.tile([128, 512], BF16, f"mstg{j}") for j in range(4)]
                sq = [S.tile([128, 512], BF16, f"sq{j}") for j in range(2)]
                wkr = S.tile([128, KC, 256], BF16, "wkr")
                S.dma(pool, wkr, m_wkr, in_ap=m_wkr.ap.rearrange("(c p) n -> p c n", p=128), sem_tile=wkr)
                t1 = S.tile([128, 512], F32, "t1")
                t2 = S.tile([128, 512], F32, "t2")
                krf = S.tile([128, 512], BF16, "krf")
                slabs = [(m_wc, 0), (m_wc, 512), (m_wz, 0), (m_wz, 512), (m_wz, 1024), (m_wz, 1536)]

                def load_slab(s):
                    w = wsl[s % 2]
                    srcd, c0 = slabs[s]
                    src = srcd.ap[:, c0:c0 + 512].rearrange("(c p) n -> p c n", p=128)
                    for hh in range(2):
                        S.dma(pool, w, srcd, out_ap=w.ap[:, hh * 8:(hh + 1) * 8, :], in_ap=src[:, hh * 8:(hh + 1) * 8, :], sem_tile=w)

                load_slab(0)
                S.dma(pool, sinS, cs_s, in_ap=cs_s.ap[0], sem_tile=sinS)
                S.dma(pool, cosT, cs_s, in_ap=cs_s.ap[1], sem_tile=cosT)
                S.dma(pool, gcol, m_gcol, sem_tile=gcol)
                for tb in range(4):
                    ts = slice(tb * 512, (tb + 1) * 512)
                    pa, pbb = PB[0], PB[1]
                    mm(pa, pa.ap, [(wkr.ap[:, kc, 0:128], xT.ap[:, kc, ts]) for kc in range(KC)], [wkr, xT])
                    mm(pbb, pbb.ap, [(wkr.ap[:, kc, 128:256], xT.ap[:, kc, ts]) for kc in range(KC)], [wkr, xT])
                    S.op(dve, (lambda ts=ts, t1=t1: V.tensor_tensor(t1.ap, PB[0].ap, cosT.ap[:, ts], ALU.mult)), [PB[0], cosT], [t1])
                    S.op(dve, (lambda ts=ts, t2=t2: V.tensor_tensor(t2.ap, PB[1].ap, sinS.ap[:, ts], ALU.mult)), [PB[1], sinS], [t2])
                    S.op(pool, (lambda ts=ts, t1=t1, t2=t2: G.tensor_tensor(kr_lo.ap[0:64, ts], t1.ap[0:64, :], t2.ap[0:64, :], ALU.add)), [t1, t2], [kr_lo])
                    S.op(pool, (lambda ts=ts, t1=t1, t2=t2: G.tensor_tensor(kr_hi.ap[64:128, ts], t1.ap[64:128, :], t2.ap[64:128, :], ALU.add)), [t1, t2], [kr_hi])
                cnt = 0
                for s in range(6):
                    if s + 1 < 6:
                        load_slab(s + 1)
                    w = wsl[s % 2]
                    if s < 2:
                        rdst = rq_bc if s == 0 else rkv_bc

                        def ssq_flush(p):
                            ssb, sqt, m, ts, rdst_ = p
                            S.op(pe, (lambda: PE_.matmul(ssb.ap, ones_b.ap, sqt.ap, start=(m == 0), stop=(m == 3))),
                                 [ones_b, sqt], [ssb], sig=(m == 3))
                            if m == 3:
                                S.op(act, (lambda: A.activation(rdst_.ap[:, ts], ssb.ap, AF.Ln, bias=RMS_EPS_AP.ap, scale=1.0 / 512.0)), [ssb, RMS_EPS_AP], [rdst_])
                                S.op(act, (lambda: A.activation(rdst_.ap[:, ts], rdst_.ap[:, ts], AF.Exp, scale=-0.5)), [rdst_], [rdst_])

                        spend = None
                        for tb in range(4):
                            ts = slice(tb * 512, (tb + 1) * 512)
                            ssb = PB[4 + (tb % 2)]
                            for m in range(4):
                                pb = PB[cnt % 4]
                                sqt = sq[cnt % 2]
                                cnt += 1
                                mm(pb, pb.ap, [(w.ap[:, kc, m * 128:(m + 1) * 128], xT.ap[:, kc, ts]) for kc in range(KC)], [w, xT])
                                gi = s * 4 + m
                                S.op(act, (lambda pb=pb, gi=gi, ts=ts: A.activation(cgT.ap[:, gi, ts], pb.ap, AF.Copy, scale=gcol.ap[:, gi:gi + 1])), [pb, gcol], [cgT])
                                S.op(act, (lambda pb=pb, sqt=sqt: A.activation(sqt.ap, pb.ap, AF.Square)), [pb], [sqt])
                                if spend is not None:
                                    ssq_flush(spend)
                                spend = (ssb, sqt, m, ts, rdst)
                        ssq_flush(spend)
                    else:
                        for m in range(4):
                            for tb in range(4):
                                ts = slice(tb * 512, (tb + 1) * 512)
                                pb = PB[cnt % 4]
                                st = stg[cnt % 4]
                                cnt += 1
                                mm(pb, pb.ap, [(w.ap[:, kc, m * 128:(m + 1) * 128], xT.ap[:, kc, ts]) for kc in range(KC)], [w, xT])
                                S.op(act, (lambda pb=pb, st=st: A.activation(st.ap, pb.ap, AF.Silu)), [pb], [st])
                                f0 = (s - 2) * 512 + m * 128
                                S.dma(sp, zsT_s, st, out_ap=zsT_s.ap[f0:f0 + 128, ts], sem_tile=st)
                pb = PB[6]
                for i in range(NT):
                    S.op(pe, (lambda i=i, pb=pb: PE_.matmul(pb.ap[:, i:i + 1], rkv_bc.ap[:, i * 128:(i + 1) * 128], ident_f.ap[:, 0:1], start=True, stop=True)),
                         [rkv_bc, ident_f], [pb], sig=(i == NT - 1))
                S.op(dve, (lambda pb=pb: V.tensor_copy(rkv_col.ap, pb.ap[:, 0:NT])), [pb], [rkv_col])
                S.barrier()
                S.release_dma_sems(wsl + stg + [wkr])
            xstack.close()
            S.stack = l1
            gw = [S.tile([128, 8192], BF16, f"gw{j}") for j in range(2)]
            wpre2 = S.tile([128, KC, 512], BF16, "wpre2")
            wout1_pre = [Tile(gw[0].ap.rearrange("p (c n) -> p c n", c=16), "wo1_0"),
                         Tile(gw[1].ap.rearrange("p (c n) -> p c n", c=16), "wo1_1"), wpre2]
            with ExitStack() as m2:
                S.stack = m2
                maskb = S.tile([128, 4 * 512], BF16, "mask")
                S.dma(pool, maskb, c_mask, sem_tile=maskb)
                NG = 2
                def vw(j, lo, hi, nm):
                    return Tile(gw[j].ap[:, lo:hi].rearrange("p (c n) -> p c n", c=4), nm)

                wqn = [vw(j, 0, 2048, "wqn") for j in range(NG)]
                wqr = [vw(j, 2048, 3072, "wqr") for j in range(NG)]
                wqp = [vw(j, 3072, 4096, "wqp") for j in range(NG)]
                wuk = [vw(j, 4096, 6144, "wuk") for j in range(NG)]
                wuv = [vw(j, 6144, 8192, "wuv") for j in range(NG)]

                def free_group_buf(j):
                    for t_ in (wqn[j], wqr[j], wqp[j], wuk[j], wuv[j]):
                        for v_ in list(t_.w.values()) + list(t_.r.values()):
                            pool.wait_for(v_)
                qn = [S.tile([128, T], BF16, f"qn{j}") for j in range(4)]
                kn = [S.tile([128, T], BF16, f"kn{j}") for j in range(4)]
                qr = [S.tile([128, T], BF16, f"qr{j}") for j in range(2)]
                vg = S.tile([128, NT, 512], BF16, "vg")
                pT = [S.tile([128, 512], BF16, f"pT{j}") for j in range(4)]
                t1 = [S.tile([128, 512], F32, f"t1_{j}") for j in range(2)]
                t2 = [S.tile([128, 512], F32, f"t2_{j}") for j in range(2)]
                rs = [S.tile([128, 512], F32, f"rs{j}") for j in range(2)]
                of = [S.tile([128, 512], F32, f"of{j}") for j in range(2)]
                zt = [S.tile([128, 512], BF16, f"zt{j}") for j in range(2)]
                ost = [S.tile([128, 512], BF16, f"ost{j}") for j in range(2)]

                def load_group(g):
                    b = g % NG
                    r4 = lambda d: d.ap.rearrange("(c p) n -> p c n", p=128)
                    S.dma(pool, wqn[b], m_wuqn, in_ap=r4(m_wuqn)[:, :, g * 512:(g + 1) * 512], sem_tile=wqn[b])
                    S.dma(pool, wqr[b], m_wuqr, in_ap=r4(m_wuqr)[:, :, g * 256:(g + 1) * 256], sem_tile=wqr[b])
                    S.dma(pool, wqp[b], m_wuqp, in_ap=r4(m_wuqp)[:, :, g * 256:(g + 1) * 256], sem_tile=wqp[b])
                    S.dma(pool, wuk[b], m_wuk, in_ap=r4(m_wuk)[:, :, g * 512:(g + 1) * 512], sem_tile=wuk[b])
                    S.dma(pool, wuv[b], m_wuv, in_ap=r4(m_wuv)[:, :, g * 512:(g + 1) * 512], sem_tile=wuv[b])

                load_group(0)
                cnt = 0
                zc = 0
                for g in range(4):
                    if g + 1 < 4:
                        load_group(g + 1)
                    else:
                        free_group_buf(0)
                        load_wout(m_wout, wout1_pre, slabs=(0, 2))
                    b = g % NG
                    for hh in range(4):
                        for tb in range(4):
                            ts = slice(tb * 512, (tb + 1) * 512)
                            pb = PB[cnt % 7]
                            cnt += 1
                            mm(pb, pb.ap, [(wqn[b].ap[:, kc, hh * 128:(hh + 1) * 128], cgT.ap[:, kc, ts]) for kc in range(4)], [wqn[b], cgT])
                            S.op(dve, (lambda pb=pb, hh=hh, ts=ts: V.scalar_tensor_tensor(qn[hh].ap[:, ts], pb.ap, QSCALE, rq_bc.ap[:, ts], ALU.mult, ALU.mult)), [pb, rq_bc], [qn[hh]])
                            pb = PB[cnt % 7]
                            cnt += 1
                            mm(pb, pb.ap, [(wuk[b].ap[:, kc, hh * 128:(hh + 1) * 128], cgT.ap[:, 4 + kc, ts]) for kc in range(4)], [wuk[b], cgT])
                            S.op(dve, (lambda pb=pb, hh=hh, ts=ts: V.tensor_tensor(kn[hh].ap[:, ts], pb.ap, rkv_bc.ap[:, ts], ALU.mult)), [pb, rkv_bc], [kn[hh]])
                    for i in range(NT):
                        pb = PB[cnt % 7]
                        cnt += 1
                        mm(pb, pb.ap, [(cgT.ap[:, 4 + kc, i * 128:(i + 1) * 128], wuv[b].ap[:, kc, :]) for kc in range(4)], [wuv[b], cgT])
                        S.op(act, (lambda pb=pb, i=i: A.activation(vg.ap[:, i, :], pb.ap, AF.Copy, scale=rkv_col.ap[:, i:i + 1])), [pb, rkv_col], [vg])
                    for pp in range(2):
                        for tb in range(4):
                            ts = slice(tb * 512, (tb + 1) * 512)
                            pa = PB[cnt % 7]
                            pbb = PB[(cnt + 1) % 7]
                            cnt += 2
                            ta, tb_ = t1[(pp * 4 + tb) % 2], t2[(pp * 4 + tb) % 2]
                            mm(pa, pa.ap, [(wqr[b].ap[:, kc, pp * 128:(pp + 1) * 128], cgT.ap[:, kc, ts]) for kc in range(4)], [wqr[b], cgT])
                            mm(pbb, pbb.ap, [(wqp[b].ap[:, kc, pp * 128:(pp + 1) * 128], cgT.ap[:, kc, ts]) for kc in range(4)], [wqp[b], cgT])
                            S.op(dve, (lambda ts=ts, pa=pa, ta=ta: V.tensor_tensor(ta.ap, pa.ap, cosT.ap[:, ts], ALU.mult)), [pa, cosT], [ta])
                            S.op(dve, (lambda ts=ts, pbb=pbb, tb_=tb_: V.tensor_tensor(tb_.ap, pbb.ap, sinS.ap[:, ts], ALU.mult)), [pbb, sinS], [tb_])
                            S.op(pool, (lambda ta=ta, tb_=tb_: G.tensor_tensor(ta.ap, ta.ap, tb_.ap, ALU.add)), [ta, tb_], [ta])
                            S.op(dve, (lambda pp=pp, ts=ts, ta=ta: V.scalar_tensor_tensor(qr[pp].ap[:, ts], ta.ap, QSCALE, rq_bc.ap[:, ts], ALU.mult, ALU.mult)), [ta, rq_bc], [qr[pp]])
                    cnt = 0
                    if g == 3:
                        free_group_buf(1)
                        load_wout(m_wout, wout1_pre, slabs=(1,))
                    blocks = [(hh, qb, kb) for hh in range(4) for qb in range(4) for kb in range(4 * qb + 4)]
                    LAG = 2
                    binfo = {}

                    def issue_score(n):
                        hh, qb, kb = blocks[n]
                        head = g * 4 + hh
                        krt = kr_lo if hh % 2 == 0 else kr_hi
                        qrt = qr[hh // 2]
                        qs = slice(qb * 512, (qb + 1) * 512)
                        ks = slice(kb * 128, (kb + 1) * 128)
                        if kb == 0:
                            zi = (hh * 4 + qb) % 2
                            S.dma(pool, zt[zi], zsT_s, in_ap=zsT_s.ap[head * 128:(head + 1) * 128, qs], sem_tile=zt[zi])
                        pb = PB[n % 3]
                        p_t = pT[n % 4]
                        j = kb - 4 * qb
                        q0 = 128 * j if j > 0 else 0
                        nq = 512 - q0
                        qv = slice(qb * 512 + q0, (qb + 1) * 512)
                        pairs = [(kn[hh].ap[:, ks], qn[hh].ap[:, qv]), (krt.ap[:, ks], qrt.ap[:, qv])]
                        rd = [kn[hh], qn[hh], krt, qrt]
                        if j >= 0:
                            pairs.append((ident_b.ap, maskb.ap[:, j * 512 + q0:(j + 1) * 512]))
                            rd = rd + [ident_b, maskb]
                        mm(pb, pb.ap[:, 0:nq], pairs, rd)
                        S.op(act, (lambda pb=pb, p_t=p_t, nq=nq: A.activation(p_t.ap[:, 0:nq], pb.ap[:, 0:nq], AF.Exp)), [pb], [p_t])
                        binfo[n] = (p_t, q0)

                    def issue_pv(n):
                        hh, qb, kb = blocks[n]
                        head = g * 4 + hh
                        nkb = 4 * qb + 4
                        ai = (hh * 4 + qb) % 2
                        ob, sb_ = PB[3 + 2 * ai], PB[4 + 2 * ai]
                        p_t, q0 = binfo.pop(n)
                        nq = 512 - q0
                        S.op(pe, (lambda: PE_.matmul(ob.ap[:, q0:512], vg.ap[:, kb, hh * 128:(hh + 1) * 128], p_t.ap[:, 0:nq], start=(kb == 0), stop=(kb == nkb - 1))),
                             [vg, p_t], [ob], sig=False)
                        S.op(pe, (lambda: PE_.matmul(sb_.ap[:, q0:512], ones_b.ap, p_t.ap[:, 0:nq], start=(kb == 0), stop=(kb == nkb - 1))),
                             [ones_b, p_t], [sb_], sig=True)
                        if kb == nkb - 1:
                            ob.w = dict(sb_.w)
                            rs_, of_, zt_, os_ = rs[ai], of[ai], zt[ai], ost[ai]
                            S.op(dve, (lambda: V.reciprocal(rs_.ap, sb_.ap)), [sb_], [rs_])
                            S.op(dve, (lambda: V.tensor_tensor(of_.ap, ob.ap, rs_.ap, ALU.mult)), [ob, rs_], [of_])
                            S.op(pool, (lambda: G.tensor_tensor(os_.ap, of_.ap, zt_.ap, ALU.mult)), [of_, zt_], [os_])
                            S.dma(sp, ogT_s, os_, out_ap=ogT_s.ap[4 * qb:4 * qb + 4, :, head, :].rearrange("i p t -> p i t"),
                                  in_ap=os_.ap.rearrange("p (i t) -> p i t", i=4), sem_tile=os_)

                    for n in range(len(blocks) + LAG):
                        if n < len(blocks):
                            issue_score(n)
                        if n - LAG >= 0:
                            issue_pv(n - LAG)
                S.barrier()
                S.release_dma_sems([maskb] + wqn + wqr + wqp + wuk + wuv + zt + ost)
            S.stack = l1
            outproj_phase(m_wout, x1_s, 1, out_d, pre=wout1_pre)
            S.stack = l1
        S.stack = top
        S.barrier()
        S.emit()
    return nc


def _consts():
    ident = np.eye(128, dtype=np.float32)
    s = np.arange(128)[:, None]
    t = np.arange(128)[None, :]
    W = np.zeros((128, 128), np.float32)
    W[(t < 64) & (s > t) & (s < 64)] = 1.0
    W[(t >= 64) & (s >= 64) & (s <= t)] = -1.0
    U128 = (s > t).astype(np.float32)
    m01 = ((s // 64) <= (t // 64)).astype(np.float32)
    U = np.concatenate([W, U128, m01], axis=1)
    ind = np.zeros((128, 3), np.float32)
    ind[:64, 0] = 1.0
    ind[64:, 1] = 1.0
    ind[:, 2] = 1.0
    mask = np.zeros((128, 4 * 512), np.float32)
    kk = np.arange(128)[:, None]
    qq = np.arange(512)[None, :]
    for j in range(4):
        vis = ((j * 128 + kk) // 64) <= (qq // 64)
        mask[:, j * 512:(j + 1) * 512] = np.where(vis, 0.0, MASKNEG)
    half = 32
    inv = (np.float32(10000.0) ** (-np.arange(half, dtype=np.float32) / np.float32(half))).astype(np.float32)
    rope = np.zeros((128, 2), np.float32)
    p = np.arange(128)
    rope[:, 0] = inv[p % 32]
    rope[:, 1] = np.where((p % 64) < 32, -1.0, 1.0)
    return ident, U, ind, mask, rope


_NC_CACHE = {}


def _prep_shared(gla_w_in, gla_w_a2, gla_b_a, gla_g_out, gla_w_out, mla_w_in, mla_g_q, mla_w_uq, mla_g_kv,
                 mla_w_ukv, mla_w_out, ln_g, ln_b):
    f = lambda a: np.ascontiguousarray(np.asarray(a, dtype=np.float32))
    gw = f(gla_w_in[0])
    g_win = f(gw[:, :6144])
    g_walr = np.zeros((D, 128), np.float32)
    g_walr[:, :16] = gw[:, 6144:6160]
    g_wa2 = np.zeros((128, 1024), np.float32)
    g_wa2[:16] = f(gla_w_a2[0])
    g_wa2[32] = f(gla_b_a[0])
    g_gout = f(np.broadcast_to(f(gla_g_out[0]).reshape(1, 2048), (128, 2048)))
    mw = f(mla_w_in[0])
    m_wc = f(mw[:, :1024])
    kr = mw[:, 1024:1088]
    krp = np.concatenate([kr[:, 32:], kr[:, :32]], axis=1)
    m_wkr = f(np.concatenate([kr, kr, krp, krp], axis=1))
    m_wz = f(mw[:, 1088:])
    m_gcol = f(np.concatenate([f(mla_g_q[0]).reshape(4, 128).T, f(mla_g_kv[0]).reshape(4, 128).T], axis=1))
    uq = f(mla_w_uq[0]).reshape(512, 16, 192)
    m_wuqn = f(uq[:, :, :128].reshape(512, 2048))
    rope = uq[:, :, 128:]
    m_wuqr = f(rope.reshape(512, 1024))
    m_wuqp = f(np.concatenate([rope[:, :, 32:], rope[:, :, :32]], axis=2).reshape(512, 1024))
    ukv = f(mla_w_ukv[0]).reshape(512, 16, 256)
    m_wuk = f(ukv[:, :, :128].reshape(512, 2048))
    m_wuv = f(ukv[:, :, 128:].reshape(512, 2048))
    lng = f(np.broadcast_to(f(ln_g).reshape(1, 2 * D), (128, 2 * D)))
    lnb = f(np.broadcast_to(f(ln_b).reshape(1, 2 * D), (128, 2 * D)))
    ident, U, ind, mask, rope_c = _consts()
    return dict(g_win=g_win, g_walr=g_walr, g_wa2=g_wa2, g_gout=g_gout, g_wout=f(gla_w_out[0]),
                m_wc=m_wc, m_wkr=m_wkr, m_wz=m_wz, m_gcol=m_gcol, m_wuqn=m_wuqn, m_wuqr=m_wuqr, m_wuqp=m_wuqp,
                m_wuk=m_wuk, m_wuv=m_wuv, m_wout=f(mla_w_out[0]), lng_bc=lng, lnb_bc=lnb,
                c_ident=ident, c_U=U, c_ind=ind, c_mask=mask, c_rope=rope_c)


def kernel(x, positions, gla_w_in, gla_w_a2, gla_b_a, gla_g_out, gla_w_out, mla_w_in, mla_g_q, mla_w_uq,
           mla_g_kv, mla_w_ukv, mla_w_out, ln_g, ln_b, _debug=False, _cores=8, _stop=None):
    x = np.asarray(x, dtype=np.float32)
    positions = np.asarray(positions).astype(np.int32)
    shared = _prep_shared(gla_w_in, gla_w_a2, gla_b_a, gla_g_out, gla_w_out, mla_w_in, mla_g_q, mla_w_uq,
                          mla_g_kv, mla_w_ukv, mla_w_out, ln_g, ln_b)
    key = (bool(_debug), _stop)
    if key not in _NC_CACHE:
        _NC_CACHE[key] = build_program(debug=_debug, stop_after=_stop)
    nc = _NC_CACHE[key]
    in_maps = []
    for c in range(_cores):
        m = dict(shared)
        m["x"] = np.ascontiguousarray(x[c])
        m["pos_bc"] = np.ascontiguousarray(np.broadcast_to(positions[c].reshape(1, T), (128, T)))
        in_maps.append(m)
    res = run_bass_kernel_spmd(nc, in_maps, core_ids=list(range(_cores)))
    if _debug:
        return res.results
    return np.stack([np.asarray(r["out"], dtype=np.float32) for r in res.results], axis=0)
```

```python
import math
from contextlib import ExitStack

import numpy as np
import concourse.bass as bass
import concourse.mybir as mybir
from concourse.bass_utils import run_bass_kernel_spmd

F32 = mybir.dt.float32
BF16 = mybir.dt.bfloat16
I32 = mybir.dt.int32
AF = mybir.ActivationFunctionType
ALU = mybir.AluOpType

T = 2048
D = 2048
NT = 16
KC = 16
ALPHA = float(4 ** 0.25)
LN_EPS = 1e-5
RMS_EPS = 1e-6
TWO_PI = float(2 * np.pi)
QSCALE = float(192 ** -0.5)
MASKNEG = -30000.0
VERBOSE = False


class Tile:
    __slots__ = ("ap", "w", "r", "dsem", "name", "multi")

    def __init__(self, ap, name="", multi=False):
        self.ap = ap
        self.w = {}
        self.r = {}
        self.dsem = None
        self.name = name
        self.multi = multi


class Eng:
    def __init__(self, S, name, sem):
        self.S = S
        self.name = name
        self.sem = sem
        self.count = 0
        self.seen = {}
        self.prog = []

    def wait_for(self, ev):
        if ev is None:
            return
        sem, val = ev
        if sem is self.sem and (self.name == "pe" or not self.S.same_engine_sync):
            return
        if self.seen.get(id(sem), 0) >= val:
            return
        self.prog.append(lambda h: h.wait_ge(sem, val))
        self.seen[id(sem)] = val


class Sched:
    def __init__(self, nc, stack, same_engine_sync=True):
        self.nc = nc
        self.stack = stack
        self.same_engine_sync = same_engine_sync
        self.nsem = 0
        self.sem_pool = []
        self.dma_evs = {}
        self.pe = Eng(self, "pe", self.new_sem("pe"))
        self.act = Eng(self, "act", self.new_sem("act"))
        self.dve = Eng(self, "dve", self.new_sem("dve"))
        self.pool = Eng(self, "pool", self.new_sem("pool"))
        self.sp = Eng(self, "sp", self.new_sem("sp"))
        self.engs = [self.pe, self.act, self.dve, self.pool, self.sp]
        self.uid = 0

    def new_sem(self, name):
        self.nsem += 1
        return self.stack.enter_context(self.nc.semaphore(f"s{self.nsem}_{name}"))

    def sb(self, shape, dt, name=None, side=None):
        self.uid += 1
        if side is None:
            t = self.stack.enter_context(self.nc.sbuf_tensor(f"{name or 't'}_{self.uid}", list(shape), dt))
        else:
            t = self.stack.enter_context(self.nc.sbuf_tensor(f"{name or 't'}_{self.uid}", list(shape), dt, side=side))
        return t

    def tile(self, shape, dt, name=None, multi=False, side=None):
        h = self.sb(shape, dt, name, side)
        return Tile(h[:], name or "t", multi)

    def ps(self, shape, dt, name=None):
        self.uid += 1
        h = self.stack.enter_context(self.nc.psum_tensor(f"{name or 'p'}_{self.uid}", list(shape), dt))
        return Tile(h[:], name or "p")

    def _deps(self, eng, reads, writes):
        for t in reads:
            for v in list(t.w.values()):
                eng.wait_for(v)
        for t in writes:
            if not t.multi:
                for v in list(t.w.values()):
                    eng.wait_for(v)
            for v in list(t.r.values()):
                eng.wait_for(v)

    def _post(self, ev, reads, writes):
        for t in reads:
            t.r[id(ev[0])] = ev
        for t in writes:
            if t.multi:
                t.w[id(ev[0])] = ev
            else:
                t.w = {id(ev[0]): ev}
                t.r = {}

    def op(self, eng, fn, reads=(), writes=(), sig=True):
        if self.dead:
            return
        self._deps(eng, reads, writes)
        if sig:
            eng.count += 1
            sem = eng.sem
            eng.prog.append(lambda h: fn().then_inc(sem, 1))
            ev = (eng.sem, eng.count)
        else:
            eng.prog.append(lambda h: fn())
            ev = (eng.sem, eng.count + 1)
        self._post(ev, reads, writes)

    def dma(self, q, out_t, in_t, out_ap=None, in_ap=None, sem_tile=None):
        st = sem_tile
        if self.dead:
            return
        self._deps(q, [in_t], [out_t])
        if st.dsem is None:
            st.dsem = self.sem_pool.pop() if self.sem_pool else self.new_sem("d")
        oa = out_ap if out_ap is not None else out_t.ap
        ia = in_ap if in_ap is not None else in_t.ap
        dsem = st.dsem
        q.prog.append(lambda h: h.dma_start(out=oa, in_=ia).then_inc(dsem, 16))
        cntv = self._semcnt.get(id(dsem), 0) + 16
        self._semcnt[id(dsem)] = cntv
        ev = (dsem, cntv)
        self.dma_evs[id(dsem)] = ev
        self._post(ev, [in_t], [out_t])

    _semcnt = {}

    def release_dma_sems(self, tiles):
        for t in tiles:
            if t.dsem is not None:
                self.sem_pool.append(t.dsem)
                t.dsem = None

    def barrier(self, tag=""):
        if self.dead:
            return
        self.nbar = getattr(self, "nbar", 0) + 1
        if VERBOSE:
            print("barrier", tag, "sbuf_remaining", self.nc.sbuf_bytes_remaining, "sems", self.nsem,
                  "counts", [e.count for e in self.engs], "prog", [len(e.prog) for e in self.engs], flush=True)
        evs = [(e.sem, e.count) for e in self.engs if e.count > 0] + list(self.dma_evs.values())
        for e in self.engs:
            for ev in evs:
                if ev[0] is e.sem:
                    continue
                e.wait_for(ev)
        self.dma_evs = {}
        if self.stop_after is not None and self.nbar >= self.stop_after:
            self.dead = True

    stop_after = None
    dead = False

    def emit(self):
        nc = self.nc
        with nc.Block() as block:
            for eng, deco in ((self.pe, block.tensor), (self.act, block.scalar), (self.dve, block.vector),
                              (self.pool, block.gpsimd), (self.sp, block.sync)):
                def body(h, eng=eng):
                    for f in eng.prog:
                        f(h)
                deco(body)


class _Stop(Exception):
    pass


def build_program(debug=False, stop_after=None):
    nc = bass.Bass("TRN2", target_bir_lowering=False)
    Sched._semcnt = {}

    def din(name, shape, dt=F32):
        return Tile(nc.dram_tensor(name, list(shape), dt, kind="ExternalInput").ap(), name)

    def dscr(name, shape, dt, out=False):
        kind = "ExternalOutput" if (out or debug) else "Internal"
        return Tile(nc.dram_tensor(name, list(shape), dt, kind=kind).ap(), name, multi=True)

    x_d = din("x", [T, D])
    pos_d = din("pos_bc", [128, T], I32)
    g_win = din("g_win", [D, 6144])
    g_walr = din("g_walr", [D, 128])
    g_wa2 = din("g_wa2", [128, 1024])
    g_gout = din("g_gout", [128, 2048])
    g_wout = din("g_wout", [D, D])
    m_wc = din("m_wc", [D, 1024])
    m_wkr = din("m_wkr", [D, 256])
    m_wz = din("m_wz", [D, 2048])
    m_gcol = din("m_gcol", [128, 8])
    m_wuqn = din("m_wuqn", [512, 2048])
    m_wuqr = din("m_wuqr", [512, 1024])
    m_wuqp = din("m_wuqp", [512, 1024])
    m_wuk = din("m_wuk", [512, 2048])
    m_wuv = din("m_wuv", [512, 2048])
    m_wout = din("m_wout", [D, D])
    lng_d = din("lng_bc", [128, 2 * D])
    lnb_d = din("lnb_bc", [128, 2 * D])
    c_ident = din("c_ident", [128, 128])
    c_U = din("c_U", [128, 384])
    c_ind = din("c_ind", [128, 3])
    c_mask = din("c_mask", [128, 4 * 512])
    c_rope = din("c_rope", [128, 2])
    out_d = Tile(nc.dram_tensor("out", [T, D], F32, kind="ExternalOutput").ap(), "out", multi=True)

    k_s = dscr("k_s", [T, 1024], BF16)
    v_s = dscr("v_s", [T, 2048], BF16)
    zs_s = dscr("zs_s", [T, 2048], BF16)
    qT_s = dscr("qT_s", [1024, T], BF16)
    x1_s = dscr("x1_s", [T, D], F32)
    zsT_s = dscr("zsT_s", [2048, T], BF16)
    ogT_s = dscr("ogT_s", [NT, 128, 16, 128], BF16)
    cs_s = dscr("cs_s", [2, 128, T], F32)

    with ExitStack() as top:
        S = Sched(nc, top, same_engine_sync=True)
        S.stop_after = stop_after
        pe, act, dve, pool, sp = S.pe, S.act, S.dve, S.pool, S.sp
        V, A, G, PE_ = nc.vector, nc.scalar, nc.gpsimd, nc.tensor

        PB = [S.ps([128, 512], F32, f"pb{i}") for i in range(7)]
        PT = S.ps([128, 512], BF16, "pt")
        PT2 = Tile(PB[6].ap.bitcast(BF16)[:, 0:512], "pt2")
        PTs = [PT, PT2]

        ident_b = S.tile([128, 128], BF16, "identb")
        ident_f = S.tile([128, 128], F32, "identf")
        ones_b = S.tile([128, 128], BF16, "onesb")
        ones_f = S.tile([128, 128], F32, "onesf")
        S.dma(pool, ident_b, c_ident, sem_tile=ident_b)
        S.dma(pool, ident_f, c_ident, sem_tile=ident_f)
        S.op(dve, lambda: V.memset(ones_b.ap, 1.0), [], [ones_b])
        S.op(dve, lambda: V.memset(ones_f.ap, 1.0), [], [ones_f])

        def mm(out_t, out_ap, pairs, reads):
            n = len(pairs)
            for i, (l, r) in enumerate(pairs):
                S.op(pe, (lambda l=l, r=r, i=i: PE_.matmul(out_ap, l, r, start=(i == 0), stop=(i == n - 1))),
                     reads, [out_t], sig=(i == n - 1))

        def xT_views(xT):
            return [Tile(xT.ap[:, :, i * 128:(i + 1) * 128], f"xT{i}", multi=True) for i in range(NT)]

        def xT_tile(src_d, xT, xTt, xin, i, use_dve=True):
            xb = xin[i % len(xin)]
            S.dma(pool, xb, src_d, in_ap=src_d.ap[i * 128:(i + 1) * 128, :], sem_tile=xb)
            for j in range(4):
                ptile = PTs[j % 2]
                for q in range(4):
                    kc = 4 * j + q
                    S.op(pe, (lambda kc=kc, q=q, xb=xb, ptile=ptile: PE_.transpose(ptile.ap[:, q * 128:(q + 1) * 128], xb.ap[:, kc * 128:(kc + 1) * 128], ident_b.ap)),
                         [xb, ident_b], [ptile], sig=(q == 3))
                dst = xT.ap[:, 4 * j:4 * j + 4, i * 128:(i + 1) * 128]
                src = ptile.ap.rearrange("p (a b) -> p a b", a=4)
                if j % 2 == 0 and use_dve:
                    S.op(dve, (lambda dst=dst, src=src: V.tensor_copy(dst, src)), [ptile], [xTt[i]])
                else:
                    S.op(act, (lambda dst=dst, src=src: A.copy(dst, src)), [ptile], [xTt[i]])

        def build_xT(src_d, xT, xTt):
            xin = [S.tile([128, D], BF16, f"xin{j}") for j in range(2)]
            for i in range(NT):
                xT_tile(src_d, xT, xTt, xin, i)
            S.release_dma_sems(xin)

        def layer_norm_tile(r, lng, lnb, outt, stats, mv, rstd):
            for s in range(4):
                S.op(dve, (lambda s=s: V.bn_stats(stats.ap[:, s * 6:(s + 1) * 6], r.ap[:, s * 512:(s + 1) * 512])), [r], [stats])
            S.op(dve, lambda: V.bn_aggr(mv.ap, stats.ap), [stats], [mv])
            S.op(act, lambda: A.activation(rstd.ap, mv.ap[:, 1:2], AF.Ln, bias=LN_EPS_AP.ap), [mv, LN_EPS_AP], [rstd])
            S.op(act, lambda: A.activation(rstd.ap, rstd.ap, AF.Exp, scale=-0.5), [rstd], [rstd])
            S.op(dve, lambda: V.scalar_tensor_tensor(outt.ap, r.ap, mv.ap[:, 0:1], lng.ap, ALU.subtract, ALU.mult), [r, mv, lng], [outt])
            S.op(dve, lambda: V.scalar_tensor_tensor(outt.ap, outt.ap, rstd.ap, lnb.ap, ALU.mult, ALU.add), [outt, rstd, lnb], [outt])

        LN_EPS_AP = S.tile([128, 1], F32, "lneps")
        S.op(dve, lambda: V.memset(LN_EPS_AP.ap, LN_EPS), [], [LN_EPS_AP])
        RMS_EPS_AP = S.tile([128, 1], F32, "rmseps")
        S.op(dve, lambda: V.memset(RMS_EPS_AP.ap, RMS_EPS), [], [RMS_EPS_AP])

        def alloc_wout():
            return [S.tile([128, KC, 512], BF16, f"wout{n}") for n in range(4)]

        def load_wout(w_d, ws=None, slabs=(0, 1, 2, 3)):
            ws = ws if ws is not None else alloc_wout()
            for n in slabs:
                src = w_d.ap[:, n * 512:(n + 1) * 512].rearrange("(c p) n -> p c n", p=128)
                for hh in range(2):
                    S.dma(pool, ws[n], w_d, out_ap=ws[n].ap[:, hh * 8:(hh + 1) * 8, :], in_ap=src[:, hh * 8:(hh + 1) * 8, :], sem_tile=ws[n])
            return ws

        def outproj_phase(w_src, resid_d, layer, dst_d, xT_out=None, pre=None):
            with ExitStack() as ph:
                S.stack = ph
                if pre is not None:
                    wout = list(pre) + [S.tile([128, KC, 512], BF16, f"woutl{n}") for n in range(len(pre), 4)]
                    load_wout(w_src, wout, slabs=tuple(range(len(pre), 4)))
                else:
                    wout = w_src if isinstance(w_src, list) else load_wout(w_src)
                NOB = 1 if xT_out is not None else 2
                obf = S.tile([128, D // 2], BF16, "obf") if xT_out is not None else None

                def xT_part(i):
                    ob_ = o_t[i % NOB]
                    for hf in range(2):
                        S.op(act, (lambda hf=hf: A.copy(obf.ap, ob_.ap[:, hf * 1024:(hf + 1) * 1024])), [ob_], [obf])
                        for jj in range(2):
                            j = 2 * hf + jj
                            ptile = PTs[j % 2]
                            for q in range(4):
                                kcl = 4 * jj + q
                                S.op(pe, (lambda kcl=kcl, q=q, ptile=ptile: PE_.transpose(ptile.ap[:, q * 128:(q + 1) * 128], obf.ap[:, kcl * 128:(kcl + 1) * 128], ident_b.ap)),
                                     [obf, ident_b], [ptile], sig=(q == 3))
                            dst = xT_out.ap[:, 4 * j:4 * j + 4, i * 128:(i + 1) * 128]
                            src = ptile.ap.rearrange("p (a b) -> p a b", a=4)
                            S.op(act, (lambda dst=dst, src=src: A.copy(dst, src)), [ptile], [xT_out])
                lng = S.tile([128, D], F32, "lng")
                lnb = S.tile([128, D], F32, "lnb")
                S.dma(pool, lng, lng_d, in_ap=lng_d.ap[:, layer * D:(layer + 1) * D], sem_tile=lng)
                S.dma(pool, lnb, lnb_d, in_ap=lnb_d.ap[:, layer * D:(layer + 1) * D], sem_tile=lnb)
                ogt = [S.tile([128, KC, 128], BF16, f"ogt{j}") for j in range(2)]
                x_t = [S.tile([128, D], F32, f"mx{j}") for j in range(2)]
                o_t = [S.tile([128, D], F32, f"mo{j}") for j in range(NOB)]
                r_t = S.tile([128, D], F32, "mr")
                stats = S.tile([128, 24], F32, "mstats")
                mv = S.tile([128, 2], F32, "mmv")
                rstd = S.tile([128, 1], F32, "mrstd")

                def loads3(i):
                    b = i % 2
                    S.dma(pool, ogt[b], ogT_s, in_ap=ogT_s.ap[i], sem_tile=ogt[b])
                    S.dma(pool, x_t[b], resid_d, in_ap=resid_d.ap[i * 128:(i + 1) * 128, :], sem_tile=x_t[b])

                loads3(0)
                cnt = 0
                for i in range(NT):
                    if i + 1 < NT:
                        loads3(i + 1)
                    b = i % 2
                    for n in range(4):
                        pb = PB[cnt % 4]
                        cnt += 1
                        mm(pb, pb.ap, [(ogt[b].ap[:, kc, :], wout[n].ap[:, kc, :]) for kc in range(KC)], [ogt[b], wout[n]])
                        S.op(dve, (lambda pb=pb, n=n, b=b: V.scalar_tensor_tensor(r_t.ap[:, n * 512:(n + 1) * 512], x_t[b].ap[:, n * 512:(n + 1) * 512], ALPHA, pb.ap, ALU.mult, ALU.add)),
                             [x_t[b], pb], [r_t])
                    if xT_out is not None and i >= 1:
                        xT_part(i - 1)
                    layer_norm_tile(r_t, lng, lnb, o_t[i % NOB], stats, mv, rstd)
                    S.dma(sp, dst_d, o_t[i % NOB], out_ap=dst_d.ap[i * 128:(i + 1) * 128, :], sem_tile=o_t[i % NOB])
                if xT_out is not None:
                    xT_part(NT - 1)
                S.barrier()
                S.release_dma_sems((wout if (pre is not None or not isinstance(w_src, list)) else []) + [lng, lnb] + ogt + x_t + o_t)

        xstack = ExitStack()
        with ExitStack() as l0:
            S.stack = l0
            a_lrT = S.tile([128, T], BF16, "alrT")
            S.op(dve, lambda: V.memset(a_lrT.ap, 0.0), [], [a_lrT])
            S.op(dve, lambda: V.memset(a_lrT.ap[32:33, :], 1.0), [], [a_lrT])
            with ExitStack() as p1:
                S.stack = p1
                xT = S.tile([128, KC, T], BF16, "xT", multi=True)
                xTt = xT_views(xT)
                NXB = 6
                xin = [S.tile([128, D], BF16, f"xin{j}") for j in range(NXB)]
                NSLOT = 3
                wsl = [S.tile([128, KC, 512], BF16, f"wsl{j}") for j in range(NSLOT)]
                stg = [S.tile([128, 512], BF16, f"stg{j}") for j in range(4)]
                walr = S.tile([128, KC, 128], BF16, "walr")

                def load_slab(s):
                    w = wsl[order.index(s) % NSLOT]
                    src = g_win.ap[:, s * 512:(s + 1) * 512].rearrange("(c p) n -> p c n", p=128)
                    for hh in range(2):
                        S.dma(pool, w, g_win, out_ap=w.ap[:, hh * 8:(hh + 1) * 8, :], in_ap=src[:, hh * 8:(hh + 1) * 8, :], sem_tile=w)

                order = [2, 3, 4, 5, 6, 7, 8, 9, 10, 11, 0, 1]
                xT_tile(x_d, xT, xTt, xin, 0, use_dve=False)
                load_slab(order[0])
                for i0 in range(1, NXB - 1):
                    xT_tile(x_d, xT, xTt, xin, i0, use_dve=False)
                load_slab(order[1])
                posi = S.tile([128, T], I32, "posi")
                S.dma(pool, posi, pos_d, sem_tile=posi)
                ropec = S.tile([128, 2], F32, "ropec")
                S.dma(pool, ropec, c_rope, sem_tile=ropec)
                ang = S.tile([128, T], F32, "ang")
                tk = S.tile([128, T], F32, "tk")
                ti = S.tile([128, T], I32, "ti")
                a_s = S.tile([128, T], F32, "a_s")
                a_c = S.tile([128, T], F32, "a_c")
                S.op(dve, lambda: V.tensor_copy(ang.ap, posi.ap), [posi], [ang])
                S.op(dve, lambda: V.tensor_scalar(ang.ap, ang.ap, ropec.ap[:, 0:1], None, ALU.mult), [ang, ropec], [ang])

                def reduce_to(shift, a2):
                    S.op(dve, lambda: V.tensor_scalar(a2.ap, ang.ap, shift, None, ALU.add), [ang], [a2])
                    S.op(dve, lambda: V.tensor_scalar(tk.ap, a2.ap, 1.0 / TWO_PI, None, ALU.mult), [a2], [tk])
                    S.op(dve, lambda: V.tensor_copy(ti.ap, tk.ap), [tk], [ti])
                    S.op(dve, lambda: V.tensor_copy(tk.ap, ti.ap), [ti], [tk])
                    S.op(dve, lambda: V.scalar_tensor_tensor(a2.ap, tk.ap, -TWO_PI, a2.ap, ALU.mult, ALU.add), [tk, a2], [a2])
                    S.op(dve, lambda: V.tensor_scalar(tk.ap, a2.ap, float(np.pi), -TWO_PI, ALU.is_gt, ALU.mult), [a2], [tk])
                    S.op(dve, lambda: V.tensor_tensor(a2.ap, a2.ap, tk.ap, ALU.add), [tk, a2], [a2])
                    S.op(dve, lambda: V.tensor_scalar(tk.ap, a2.ap, float(-np.pi), TWO_PI, ALU.is_lt, ALU.mult), [a2], [tk])
                    S.op(dve, lambda: V.tensor_tensor(a2.ap, a2.ap, tk.ap, ALU.add), [tk, a2], [a2])

                reduce_to(0.0, a_s)
                reduce_to(float(np.pi / 2), a_c)
                S.dma(pool, walr, g_walr, in_ap=g_walr.ap.rearrange("(c p) n -> p c n", p=128), sem_tile=walr)
                cnt = 0
                for si, s in enumerate(order):
                    if si + 2 < 12:
                        load_slab(order[si + 2])
                    w = wsl[si % NSLOT]
                    if si == 1:
                        continue
                    if s < 2:
                        for m in range(4):
                            for tb in range(4):
                                pb = PB[cnt % 4]
                                st = stg[cnt % 4]
                                cnt += 1
                                mm(pb, pb.ap, [(w.ap[:, kc, m * 128:(m + 1) * 128], xT.ap[:, kc, tb * 512:(tb + 1) * 512]) for kc in range(KC)], [w] + xTt[4 * tb:4 * tb + 4])
                                S.op(act, (lambda pb=pb, st=st: A.mul(st.ap, pb.ap, 1.0 / 16.0)), [pb], [st])
                                f0 = s * 512 + m * 128
                                S.dma(sp, qT_s, st, out_ap=qT_s.ap[f0:f0 + 128, tb * 512:(tb + 1) * 512], sem_tile=st)
                    else:
                        if s < 4:
                            dst, c0 = k_s, (s - 2) * 512
                        elif s < 8:
                            dst, c0 = v_s, (s - 4) * 512
                        else:
                            dst, c0 = zs_s, (s - 8) * 512
                        for i in range(NT):
                            if si == 0 and i + NXB - 1 < NT:
                                xT_tile(x_d, xT, xTt, xin, i + NXB - 1, use_dve=False)
                            pb = PB[cnt % 4]
                            st = stg[cnt % 4]
                            cnt += 1
                            mm(pb, pb.ap, [(xT.ap[:, kc, i * 128:(i + 1) * 128], w.ap[:, kc, :]) for kc in range(KC)], [w, xTt[i]])
                            if s >= 8:
                                S.op(act, (lambda pb=pb, st=st: A.activation(st.ap, pb.ap, AF.Silu)), [pb], [st])
                            else:
                                S.op(act, (lambda pb=pb, st=st: A.copy(st.ap, pb.ap)), [pb], [st])
                            S.dma(sp, dst, st, out_ap=dst.ap[i * 128:(i + 1) * 128, c0:c0 + 512], sem_tile=st)
                            if si == 0:
                                w1 = wsl[1 % NSLOT]
                                pb = PB[cnt % 4]
                                st = stg[cnt % 4]
                                cnt += 1
                                mm(pb, pb.ap, [(xT.ap[:, kc, i * 128:(i + 1) * 128], w1.ap[:, kc, :]) for kc in range(KC)], [w1, xTt[i]])
                                S.op(act, (lambda pb=pb, st=st: A.copy(st.ap, pb.ap)), [pb], [st])
                                c1 = (order[1] - 2) * 512
                                S.dma(sp, k_s, st, out_ap=k_s.ap[i * 128:(i + 1) * 128, c1:c1 + 512], sem_tile=st)
                for tb in range(4):
                    pb = PB[tb % 4]
                    mm(pb, pb.ap, [(walr.ap[:, kc, :], xT.ap[:, kc, tb * 512:(tb + 1) * 512]) for kc in range(KC)], [walr] + xTt[4 * tb:4 * tb + 4])
                    S.op(act, (lambda pb=pb, tb=tb: A.copy(a_lrT.ap[0:32, tb * 512:(tb + 1) * 512], pb.ap[0:32, :])), [pb], [a_lrT])
                S.op(act, lambda: A.activation(a_s.ap, a_s.ap, AF.Sin), [a_s], [a_s])
                S.op(act, lambda: A.activation(a_c.ap, a_c.ap, AF.Sin), [a_c], [a_c])
                S.op(dve, lambda: V.tensor_scalar(a_s.ap, a_s.ap, ropec.ap[:, 1:2], None, ALU.mult), [a_s, ropec], [a_s])
                S.dma(sp, cs_s, a_s, out_ap=cs_s.ap[0], sem_tile=a_s)
                S.dma(sp, cs_s, a_c, out_ap=cs_s.ap[1], sem_tile=a_c)
                S.barrier()
                S.release_dma_sems(wsl + stg + [walr, posi, ropec, a_s, a_c] + xin)
            S.stack = l0
            wout0 = alloc_wout()
            with ExitStack() as p3:
                S.stack = p3
                wa2 = S.tile([128, 1024], BF16, "wa2")
                S.dma(pool, wa2, g_wa2, sem_tile=wa2)
                Wm = S.tile([128, 128], BF16, "Wm")
                S.dma(pool, Wm, c_U, in_ap=c_U.ap[:, 0:128], sem_tile=Wm)
                m01 = S.tile([128, 128], F32, "m01")
                S.dma(pool, m01, c_U, in_ap=c_U.ap[:, 256:384], sem_tile=m01)
                ind = S.tile([128, 3], BF16, "ind")
                S.dma(pool, ind, c_ind, sem_tile=ind)
                gout = S.tile([128, 2048], F32, "gout")
                S.dma(pool, gout, g_gout, sem_tile=gout)
                state = [S.tile([128, 512], F32, f"st{j}") for j in range(8)]
                state_b = [[S.tile([128, 512], BF16, f"sb{j}_{c}") for c in range(2)] for j in range(8)]
                for j in range(8):
                    S.op(pool, (lambda j=j: G.memset(state[j].ap, 0.0)), [], [state[j]])
                NB = 2
                k_t = [S.tile([128, 1024], BF16, f"k{j}") for j in range(NB)]
                v_t = [S.tile([128, 2048], BF16, f"v{j}") for j in range(3)]
                z_t = [S.tile([128, 2048], BF16, f"z{j}") for j in range(NB)]
                q_t = [S.tile([128, 8, 128], BF16, f"q{j}") for j in range(3)]
                kk = [S.tile([128, 1024], BF16, f"kk{j}") for j in range(NB)]
                kkT = [S.tile([128, 8, 128], BF16, f"kkT{j}") for j in range(NB)]
                cc = [S.tile([128, 8], F32, f"cc{j}") for j in range(NB)]
                lp = S.tile([128, 1024], BF16, "lp")
                lpe = S.tile([128, 1024], F32, "lpe")
                Gm = S.tile([128, 1024], F32, "Gm")
                dec = [S.tile([128, 24], F32, f"dec{j}") for j in range(NB)]
                gz = [S.tile([128, 2048], BF16, f"gz{j}") for j in range(NB)]
                og = [S.tile([128, 2048], BF16, f"og{j}") for j in range(NB)]
                ogT = [S.tile([128, KC, 128], BF16, f"ogT{j}") for j in range(NB)]
                ss = [[S.tile([128, 1], F32, f"ss{j}_{h}") for h in range(4)] for j in range(NB)]
                PTm = [S.tile([128, 128], BF16, f"PTm{j}") for j in range(3)]
                junk = S.tile([128, 512], F32, "junk")
                WB = PB[0:4]
                OB = PB[4:6]
                wcnt = [0]

                def wbank():
                    bk = WB[wcnt[0] % 4]
                    wcnt[0] += 1
                    return bk

                def loads(i):
                    b = i % NB
                    b3 = i % 3
                    sl = slice(i * 128, (i + 1) * 128)
                    S.dma(pool, k_t[b], k_s, in_ap=k_s.ap[sl, :], sem_tile=k_t[b])
                    S.dma(pool, q_t[b3], qT_s, in_ap=qT_s.ap[:, sl].rearrange("(c p) t -> p c t", p=128), sem_tile=q_t[b3])
                    S.dma(pool, v_t[b3], v_s, in_ap=v_s.ap[sl, :], sem_tile=v_t[b3])
                    S.dma(pool, z_t[b], zs_s, in_ap=zs_s.ap[sl, :], sem_tile=z_t[b])

                def pre_la(i):
                    sl = slice(i * 128, (i + 1) * 128)
                    for n in range(2):
                        pb = wbank()
                        mm(pb, pb.ap, [(a_lrT.ap[:, sl], wa2.ap[:, n * 512:(n + 1) * 512])], [a_lrT, wa2])
                        S.op(act, (lambda pb=pb, n=n: A.activation(lpe.ap[:, n * 512:(n + 1) * 512], pb.ap, AF.Exp, scale=-1.0)), [pb], [lpe])
                    S.op(act, lambda: A.activation(lp.ap, lpe.ap, AF.Ln, bias=1.0), [lpe], [lp])

                def pre_D1(i):
                    for n in range(2):
                        pb = wbank()
                        mm(pb, pb.ap, [(Wm.ap, lp.ap[:, n * 512:(n + 1) * 512])], [Wm, lp])
                        S.op(act, (lambda pb=pb, n=n: A.activation(Gm.ap[:, n * 512:(n + 1) * 512], pb.ap, AF.Exp, scale=-1.0 / 16.0)), [pb], [Gm])
                    b = i % NB
                    pb = wbank()
                    for dc in range(8):
                        S.op(pe, (lambda dc=dc, pb=pb: PE_.matmul(pb.ap[:, dc * 3:dc * 3 + 3], lp.ap[:, dc * 128:(dc + 1) * 128], ind.ap, start=True, stop=True)),
                             [lp, ind], [pb], sig=(dc == 7))
                    S.op(act, (lambda pb=pb: A.activation(dec[b].ap, pb.ap[:, 0:24], AF.Exp, scale=-1.0 / 16.0)), [pb], [dec[b]])

                def pre_D2(i):
                    b = i % NB
                    S.op(dve, lambda: V.tensor_tensor(kk[b].ap, k_t[b].ap, Gm.ap, ALU.mult), [k_t[b], Gm], [kk[b]])
                    de = dec[b].ap.rearrange("p (a c) -> p a c", c=3)[:, :, 0]
                    if i == 0:
                        S.op(dve, (lambda de=de: V.tensor_copy(cc[b].ap, de)), [dec[b]], [cc[b]])
                    else:
                        bp = (i - 1) % NB
                        dop = dec[bp].ap.rearrange("p (a c) -> p a c", c=3)[:, :, 1]
                        S.op(dve, (lambda de=de, dop=dop: V.tensor_tensor(cc[b].ap, de, dop, ALU.mult)), [dec[b], dec[bp]], [cc[b]])
                    qv = q_t[i % 3].ap[:, :, 64:128]
                    dv = dec[b].ap.rearrange("p (a c) -> p a c", c=3)[:, :, 1:2].broadcast_to([128, 8, 64])
                    S.op(dve, (lambda: V.tensor_tensor(qv, qv, dv, ALU.mult)), [q_t[i % 3], dec[b]], [q_t[i % 3]])
                    S.op(pool, lambda: G.tensor_tensor(gz[b].ap, z_t[b].ap, gout.ap, ALU.mult), [z_t[b], gout], [gz[b]])

                def pre_D3(i):
                    b = i % NB
                    for j in range(2):
                        ptile = PTs[j % 2]
                        for q in range(4):
                            kc = 4 * j + q
                            S.op(pe, (lambda kc=kc, q=q, ptile=ptile: PE_.transpose(ptile.ap[:, q * 128:(q + 1) * 128], kk[b].ap[:, kc * 128:(kc + 1) * 128], ident_b.ap)),
                                 [kk[b], ident_b], [ptile], sig=(q == 3))
                        dst = kkT[b].ap[:, 4 * j:4 * j + 4, :]
                        src = ptile.ap.rearrange("p (a b) -> p a b", a=4)
                        if j == 0:
                            S.op(act, (lambda dst=dst, src=src: A.copy(dst, src)), [ptile], [kkT[b]])
                        else:
                            S.op(dve, (lambda dst=dst, src=src: V.tensor_copy(dst, src)), [ptile], [kkT[b]])

                pend = []
                OLAG = 1
                hcnt = [0]

                def back(i, h, ptm):
                    b = i % NB
                    par = i % 2
                    ob = OB[(i * 4 + h) % 2]
                    S.op(pe, (lambda: PE_.matmul(ob.ap, q_t[i % 3].ap[:, 2 * h, :], state_b[2 * h][par].ap, start=True, stop=False)),
                         [q_t[i % 3], state_b[2 * h][par]], [ob], sig=False)
                    S.op(pe, (lambda: PE_.matmul(ob.ap, q_t[i % 3].ap[:, 2 * h + 1, :], state_b[2 * h + 1][par].ap, start=False, stop=False)),
                         [q_t[i % 3], state_b[2 * h + 1][par]], [ob], sig=False)
                    S.op(pe, (lambda: PE_.matmul(ob.ap, ptm.ap, v_t[i % 3].ap[:, h * 512:(h + 1) * 512], start=False, stop=True)),
                         [ptm, v_t[i % 3]], [ob], sig=True)
                    sst = ss[b][h]
                    S.op(act, (lambda: A.activation(junk.ap, ob.ap, AF.Square, accum_out=sst.ap)), [ob], [junk, sst])
                    S.op(act, (lambda: A.activation(sst.ap, sst.ap, AF.Ln, bias=RMS_EPS_AP.ap, scale=1.0 / 512.0)), [sst, RMS_EPS_AP], [sst])
                    S.op(act, (lambda: A.activation(sst.ap, sst.ap, AF.Exp, scale=-0.5)), [sst], [sst])
                    S.op(dve, (lambda: V.scalar_tensor_tensor(og[b].ap[:, h * 512:(h + 1) * 512], ob.ap, sst.ap, gz[b].ap[:, h * 512:(h + 1) * 512], ALU.mult, ALU.mult)),
                         [ob, sst, gz[b]], [og[b]])

                def chain_h(i, h):
                    b = i % NB
                    par = i % 2
                    for j in range(2):
                        sj = 2 * h + j
                        S.op(act, (lambda sj=sj: A.activation(state_b[sj][par].ap, state[sj].ap, AF.Copy, scale=cc[b].ap[:, sj:sj + 1])),
                             [state[sj], cc[b]], [state_b[sj][par]])
                    pbp = wbank()
                    mm(pbp, pbp.ap[:, 0:128], [(kkT[b].ap[:, 2 * h + j, :], q_t[i % 3].ap[:, 2 * h + j, :]) for j in range(2)], [kkT[b], q_t[i % 3]])
                    ptm = PTm[hcnt[0] % 3]
                    hcnt[0] += 1
                    S.op(dve, (lambda pbp=pbp, ptm=ptm: V.tensor_tensor(ptm.ap, pbp.ap[:, 0:128], m01.ap, ALU.mult)), [pbp, m01], [ptm])
                    for j in range(2):
                        sj = 2 * h + j
                        pb = wbank()
                        mm(pb, pb.ap, [(kk[b].ap[:, sj * 128:(sj + 1) * 128], v_t[i % 3].ap[:, h * 512:(h + 1) * 512])], [kk[b], v_t[i % 3]])
                        S.op(dve, (lambda pb=pb, sj=sj: V.scalar_tensor_tensor(state[sj].ap, state[sj].ap, cc[b].ap[:, sj:sj + 1], pb.ap, ALU.mult, ALU.add)),
                             [state[sj], cc[b], pb], [state[sj]])
                    pend.append((i, h, ptm))
                    if len(pend) > OLAG:
                        back(*pend.pop(0))

                def otrans(i):
                    b = i % NB
                    for j in range(4):
                        ptile = PTs[j % 2]
                        for q in range(4):
                            kc = 4 * j + q
                            S.op(pe, (lambda kc=kc, q=q, ptile=ptile: PE_.transpose(ptile.ap[:, q * 128:(q + 1) * 128], og[b].ap[:, kc * 128:(kc + 1) * 128], ident_b.ap)),
                                 [og[b], ident_b], [ptile], sig=(q == 3))
                        dst = ogT[b].ap[:, 4 * j:4 * j + 4, :]
                        src = ptile.ap.rearrange("p (a b) -> p a b", a=4)
                        S.op(dve, (lambda dst=dst, src=src: V.tensor_copy(dst, src)), [ptile], [ogT[b]])
                    S.dma(sp, ogT_s, ogT[b], out_ap=ogT_s.ap[i], sem_tile=ogT[b])

                loads(0)
                loads(1)
                load_wout(g_wout, wout0)
                pre_la(0)
                pre_D1(0)
                pre_D2(0)
                pre_D3(0)
                for i in range(NT):
                    if i + 1 < NT:
                        pre_la(i + 1)
                    chain_h(i, 0)
                    if i + 2 < NT:
                        loads(i + 2)
                    if i + 1 < NT:
                        pre_D1(i + 1)
                    chain_h(i, 1)
                    if i + 1 < NT:
                        pre_D2(i + 1)
                    chain_h(i, 2)
                    if i + 1 < NT:
                        pre_D3(i + 1)
                    chain_h(i, 3)
                    if i >= 1:
                        otrans(i - 1)
                while pend:
                    back(*pend.pop(0))
                otrans(NT - 1)
                S.barrier()
                S.release_dma_sems([wa2, Wm, m01, ind, gout] + k_t + v_t + z_t + q_t + ogT)
            S.stack = xstack
            rpad = S.tile([128, 8192], BF16, "rpad", side="right")
            x1T = S.tile([128, KC, T], BF16, "x1T", multi=True, side="right")
            S.stack = l0
            outproj_phase(wout0, x_d, 0, x1_s, xT_out=x1T)
            S.release_dma_sems(wout0)
            S.stack = l0
            S.barrier()
        S.stack = top

        with ExitStack() as l1:
            S.stack = l1
            cgT = S.tile([128, 8, T], BF16, "cgT")
            rq_bc = S.tile([128, T], F32, "rq")
            rkv_bc = S.tile([128, T], F32, "rkv")
            rkv_col = S.tile([128, NT], F32, "rkvc")
            cosT = S.tile([128, T], F32, "cosT")
            sinS = S.tile([128, T], F32, "sinS")
            kr_lo = S.tile([128, T], BF16, "krlo")
            kr_hi = S.tile([128, T], BF16, "krhi")
            gcol = S.tile([128, 8], F32, "gcol")
            S.op(pool, lambda: G.memset(kr_lo.ap, 0.0), [], [kr_lo])
            S.op(pool, lambda: G.memset(kr_hi.ap, 0.0), [], [kr_hi])
            with ExitStack() as m1:
                S.stack = m1
                xT = x1T
                wsl = [S.tile([128, KC, 512], BF16, f"mw{j}") for j in range(2)]
                stg = [S.tile([128, 512], BF16, f"mstg{j}") for j in range(4)]
                sq = [S.tile([128, 512], BF16, f"sq{j}") for j in range(2)]
                wkr = S.tile([128, KC, 256], BF16, "wkr")
                S.dma(pool, wkr, m_wkr, in_ap=m_wkr.ap.rearrange("(c p) n -> p c n", p=128), sem_tile=wkr)
                t1 = S.tile([128, 512], F32, "t1")
                t2 = S.tile([128, 512], F32, "t2")
                krf = S.tile([128, 512], BF16, "krf")
                slabs = [(m_wc, 0), (m_wc, 512), (m_wz, 0), (m_wz, 512), (m_wz, 1024), (m_wz, 1536)]

                def load_slab(s):
                    w = wsl[s % 2]
                    srcd, c0 = slabs[s]
                    src = srcd.ap[:, c0:c0 + 512].rearrange("(c p) n -> p c n", p=128)
                    for hh in range(2):
                        S.dma(pool, w, srcd, out_ap=w.ap[:, hh * 8:(hh + 1) * 8, :], in_ap=src[:, hh * 8:(hh + 1) * 8, :], sem_tile=w)

                load_slab(0)
                S.dma(pool, sinS, cs_s, in_ap=cs_s.ap[0], sem_tile=sinS)
                S.dma(pool, cosT, cs_s, in_ap=cs_s.ap[1], sem_tile=cosT)
                S.dma(pool, gcol, m_gcol, sem_tile=gcol)
                for tb in range(4):
                    ts = slice(tb * 512, (tb + 1) * 512)
                    pa, pbb = PB[0], PB[1]
                    mm(pa, pa.ap, [(wkr.ap[:, kc, 0:128], xT.ap[:, kc, ts]) for kc in range(KC)], [wkr, xT])
                    mm(pbb, pbb.ap, [(wkr.ap[:, kc, 128:256], xT.ap[:, kc, ts]) for kc in range(KC)], [wkr, xT])
                    S.op(dve, (lambda ts=ts, t1=t1: V.tensor_tensor(t1.ap, PB[0].ap, cosT.ap[:, ts], ALU.mult)), [PB[0], cosT], [t1])
                    S.op(dve, (lambda ts=ts, t2=t2: V.tensor_tensor(t2.ap, PB[1].ap, sinS.ap[:, ts], ALU.mult)), [PB[1], sinS], [t2])
                    S.op(pool, (lambda ts=ts, t1=t1, t2=t2: G.tensor_tensor(kr_lo.ap[0:64, ts], t1.ap[0:64, :], t2.ap[0:64, :], ALU.add)), [t1, t2], [kr_lo])
                    S.op(pool, (lambda ts=ts, t1=t1, t2=t2: G.tensor_tensor(kr_hi.ap[64:128, ts], t1.ap[64:128, :], t2.ap[64:128, :], ALU.add)), [t1, t2], [kr_hi])
                cnt = 0
                for s in range(6):
                    if s + 1 < 6:
                        load_slab(s + 1)
                    w = wsl[s % 2]
                    if s < 2:
                        rdst = rq_bc if s == 0 else rkv_bc

                        def ssq_flush(p):
                            ssb, sqt, m, ts, rdst_ = p
                            S.op(pe, (lambda: PE_.matmul(ssb.ap, ones_b.ap, sqt.ap, start=(m == 0), stop=(m == 3))),
                                 [ones_b, sqt], [ssb], sig=(m == 3))
                            if m == 3:
                                S.op(act, (lambda: A.activation(rdst_.ap[:, ts], ssb.ap, AF.Ln, bias=RMS_EPS_AP.ap, scale=1.0 / 512.0)), [ssb, RMS_EPS_AP], [rdst_])
                                S.op(act, (lambda: A.activation(rdst_.ap[:, ts], rdst_.ap[:, ts], AF.Exp, scale=-0.5)), [rdst_], [rdst_])

                        spend = None
                        for tb in range(4):
                            ts = slice(tb * 512, (tb + 1) * 512)
                            ssb = PB[4 + (tb % 2)]
                            for m in range(4):
                                pb = PB[cnt % 4]
                                sqt = sq[cnt % 2]
                                cnt += 1
                                mm(pb, pb.ap, [(w.ap[:, kc, m * 128:(m + 1) * 128], xT.ap[:, kc, ts]) for kc in range(KC)], [w, xT])
                                gi = s * 4 + m
                                S.op(act, (lambda pb=pb, gi=gi, ts=ts: A.activation(cgT.ap[:, gi, ts], pb.ap, AF.Copy, scale=gcol.ap[:, gi:gi + 1])), [pb, gcol], [cgT])
                                S.op(act, (lambda pb=pb, sqt=sqt: A.activation(sqt.ap, pb.ap, AF.Square)), [pb], [sqt])
                                if spend is not None:
                                    ssq_flush(spend)
                                spend = (ssb, sqt, m, ts, rdst)
                        ssq_flush(spend)
                    else:
                        for m in range(4):
                            for tb in range(4):
                                ts = slice(tb * 512, (tb + 1) * 512)
                                pb = PB[cnt % 4]
                                st = stg[cnt % 4]
                                cnt += 1
                                mm(pb, pb.ap, [(w.ap[:, kc, m * 128:(m + 1) * 128], xT.ap[:, kc, ts]) for kc in range(KC)], [w, xT])
                                S.op(act, (lambda pb=pb, st=st: A.activation(st.ap, pb.ap, AF.Silu)), [pb], [st])
                                f0 = (s - 2) * 512 + m * 128
                                S.dma(sp, zsT_s, st, out_ap=zsT_s.ap[f0:f0 + 128, ts], sem_tile=st)
                pb = PB[6]
                for i in range(NT):
                    S.op(pe, (lambda i=i, pb=pb: PE_.matmul(pb.ap[:, i:i + 1], rkv_bc.ap[:, i * 128:(i + 1) * 128], ident_f.ap[:, 0:1], start=True, stop=True)),
                         [rkv_bc, ident_f], [pb], sig=(i == NT - 1))
                S.op(dve, (lambda pb=pb: V.tensor_copy(rkv_col.ap, pb.ap[:, 0:NT])), [pb], [rkv_col])
                S.barrier()
                S.release_dma_sems(wsl + stg + [wkr])
            xstack.close()
            S.stack = l1
            gw = [S.tile([128, 8192], BF16, f"gw{j}") for j in range(2)]
            wpre2 = S.tile([128, KC, 512], BF16, "wpre2")
            wout1_pre = [Tile(gw[0].ap.rearrange("p (c n) -> p c n", c=16), "wo1_0"),
                         Tile(gw[1].ap.rearrange("p (c n) -> p c n", c=16), "wo1_1"), wpre2]
            with ExitStack() as m2:
                S.stack = m2
                maskb = S.tile([128, 4 * 512], BF16, "mask")
                S.dma(pool, maskb, c_mask, sem_tile=maskb)
                NG = 2
                def vw(j, lo, hi, nm):
                    return Tile(gw[j].ap[:, lo:hi].rearrange("p (c n) -> p c n", c=4), nm)

                wqn = [vw(j, 0, 2048, "wqn") for j in range(NG)]
                wqr = [vw(j, 2048, 3072, "wqr") for j in range(NG)]
                wqp = [vw(j, 3072, 4096, "wqp") for j in range(NG)]
                wuk = [vw(j, 4096, 6144, "wuk") for j in range(NG)]
                wuv = [vw(j, 6144, 8192, "wuv") for j in range(NG)]

                def free_group_buf(j):
                    for t_ in (wqn[j], wqr[j], wqp[j], wuk[j], wuv[j]):
                        for v_ in list(t_.w.values()) + list(t_.r.values()):
                            pool.wait_for(v_)
                qn = [S.tile([128, T], BF16, f"qn{j}") for j in range(4)]
                kn = [S.tile([128, T], BF16, f"kn{j}") for j in range(4)]
                qr = [S.tile([128, T], BF16, f"qr{j}") for j in range(2)]
                vg = S.tile([128, NT, 512], BF16, "vg")
                pT = [S.tile([128, 512], BF16, f"pT{j}") for j in range(4)]
                t1 = [S.tile([128, 512], F32, f"t1_{j}") for j in range(2)]
                t2 = [S.tile([128, 512], F32, f"t2_{j}") for j in range(2)]
                rs = [S.tile([128, 512], F32, f"rs{j}") for j in range(2)]
                of = [S.tile([128, 512], F32, f"of{j}") for j in range(2)]
                zt = [S.tile([128, 512], BF16, f"zt{j}") for j in range(2)]
                ost = [S.tile([128, 512], BF16, f"ost{j}") for j in range(2)]

                def load_group(g):
                    b = g % NG
                    r4 = lambda d: d.ap.rearrange("(c p) n -> p c n", p=128)
                    S.dma(pool, wqn[b], m_wuqn, in_ap=r4(m_wuqn)[:, :, g * 512:(g + 1) * 512], sem_tile=wqn[b])
                    S.dma(pool, wqr[b], m_wuqr, in_ap=r4(m_wuqr)[:, :, g * 256:(g + 1) * 256], sem_tile=wqr[b])
                    S.dma(pool, wqp[b], m_wuqp, in_ap=r4(m_wuqp)[:, :, g * 256:(g + 1) * 256], sem_tile=wqp[b])
                    S.dma(pool, wuk[b], m_wuk, in_ap=r4(m_wuk)[:, :, g * 512:(g + 1) * 512], sem_tile=wuk[b])
                    S.dma(pool, wuv[b], m_wuv, in_ap=r4(m_wuv)[:, :, g * 512:(g + 1) * 512], sem_tile=wuv[b])

                load_group(0)
                cnt = 0
                zc = 0
                for g in range(4):
                    if g + 1 < 4:
                        load_group(g + 1)
                    else:
                        free_group_buf(0)
                        load_wout(m_wout, wout1_pre, slabs=(0, 2))
                    b = g % NG
                    for hh in range(4):
                        for tb in range(4):
                            ts = slice(tb * 512, (tb + 1) * 512)
                            pb = PB[cnt % 7]
                            cnt += 1
                            mm(pb, pb.ap, [(wqn[b].ap[:, kc, hh * 128:(hh + 1) * 128], cgT.ap[:, kc, ts]) for kc in range(4)], [wqn[b], cgT])
                            S.op(dve, (lambda pb=pb, hh=hh, ts=ts: V.scalar_tensor_tensor(qn[hh].ap[:, ts], pb.ap, QSCALE, rq_bc.ap[:, ts], ALU.mult, ALU.mult)), [pb, rq_bc], [qn[hh]])
                            pb = PB[cnt % 7]
                            cnt += 1
                            mm(pb, pb.ap, [(wuk[b].ap[:, kc, hh * 128:(hh + 1) * 128], cgT.ap[:, 4 + kc, ts]) for kc in range(4)], [wuk[b], cgT])
                            S.op(dve, (lambda pb=pb, hh=hh, ts=ts: V.tensor_tensor(kn[hh].ap[:, ts], pb.ap, rkv_bc.ap[:, ts], ALU.mult)), [pb, rkv_bc], [kn[hh]])
                    for i in range(NT):
                        pb = PB[cnt % 7]
                        cnt += 1
                        mm(pb, pb.ap, [(cgT.ap[:, 4 + kc, i * 128:(i + 1) * 128], wuv[b].ap[:, kc, :]) for kc in range(4)], [wuv[b], cgT])
                        S.op(act, (lambda pb=pb, i=i: A.activation(vg.ap[:, i, :], pb.ap, AF.Copy, scale=rkv_col.ap[:, i:i + 1])), [pb, rkv_col], [vg])
                    for pp in range(2):
                        for tb in range(4):
                            ts = slice(tb * 512, (tb + 1) * 512)
                            pa = PB[cnt % 7]
                            pbb = PB[(cnt + 1) % 7]
                            cnt += 2
                            ta, tb_ = t1[(pp * 4 + tb) % 2], t2[(pp * 4 + tb) % 2]
                            mm(pa, pa.ap, [(wqr[b].ap[:, kc, pp * 128:(pp + 1) * 128], cgT.ap[:, kc, ts]) for kc in range(4)], [wqr[b], cgT])
                            mm(pbb, pbb.ap, [(wqp[b].ap[:, kc, pp * 128:(pp + 1) * 128], cgT.ap[:, kc, ts]) for kc in range(4)], [wqp[b], cgT])
                            S.op(dve, (lambda ts=ts, pa=pa, ta=ta: V.tensor_tensor(ta.ap, pa.ap, cosT.ap[:, ts], ALU.mult)), [pa, cosT], [ta])
                            S.op(dve, (lambda ts=ts, pbb=pbb, tb_=tb_: V.tensor_tensor(tb_.ap, pbb.ap, sinS.ap[:, ts], ALU.mult)), [pbb, sinS], [tb_])
                            S.op(pool, (lambda ta=ta, tb_=tb_: G.tensor_tensor(ta.ap, ta.ap, tb_.ap, ALU.add)), [ta, tb_], [ta])
                            S.op(dve, (lambda pp=pp, ts=ts, ta=ta: V.scalar_tensor_tensor(qr[pp].ap[:, ts], ta.ap, QSCALE, rq_bc.ap[:, ts], ALU.mult, ALU.mult)), [ta, rq_bc], [qr[pp]])
                    cnt = 0
                    if g == 3:
                        free_group_buf(1)
                        load_wout(m_wout, wout1_pre, slabs=(1,))
                    blocks = [(hh, qb, kb) for hh in range(4) for qb in range(4) for kb in range(4 * qb + 4)]
                    LAG = 2
                    binfo = {}

                    def issue_score(n):
                        hh, qb, kb = blocks[n]
                        head = g * 4 + hh
                        krt = kr_lo if hh % 2 == 0 else kr_hi
                        qrt = qr[hh // 2]
                        qs = slice(qb * 512, (qb + 1) * 512)
                        ks = slice(kb * 128, (kb + 1) * 128)
                        if kb == 0:
                            zi = (hh * 4 + qb) % 2
                            S.dma(pool, zt[zi], zsT_s, in_ap=zsT_s.ap[head * 128:(head + 1) * 128, qs], sem_tile=zt[zi])
                        pb = PB[n % 3]
                        p_t = pT[n % 4]
                        j = kb - 4 * qb
                        q0 = 128 * j if j > 0 else 0
                        nq = 512 - q0
                        qv = slice(qb * 512 + q0, (qb + 1) * 512)
                        pairs = [(kn[hh].ap[:, ks], qn[hh].ap[:, qv]), (krt.ap[:, ks], qrt.ap[:, qv])]
                        rd = [kn[hh], qn[hh], krt, qrt]
                        if j >= 0:
                            pairs.append((ident_b.ap, maskb.ap[:, j * 512 + q0:(j + 1) * 512]))
                            rd = rd + [ident_b, maskb]
                        mm(pb, pb.ap[:, 0:nq], pairs, rd)
                        S.op(act, (lambda pb=pb, p_t=p_t, nq=nq: A.activation(p_t.ap[:, 0:nq], pb.ap[:, 0:nq], AF.Exp)), [pb], [p_t])
                        binfo[n] = (p_t, q0)

                    def issue_pv(n):
                        hh, qb, kb = blocks[n]
                        head = g * 4 + hh
                        nkb = 4 * qb + 4
                        ai = (hh * 4 + qb) % 2
                        ob, sb_ = PB[3 + 2 * ai], PB[4 + 2 * ai]
                        p_t, q0 = binfo.pop(n)
                        nq = 512 - q0
                        S.op(pe, (lambda: PE_.matmul(ob.ap[:, q0:512], vg.ap[:, kb, hh * 128:(hh + 1) * 128], p_t.ap[:, 0:nq], start=(kb == 0), stop=(kb == nkb - 1))),
                             [vg, p_t], [ob], sig=False)
                        S.op(pe, (lambda: PE_.matmul(sb_.ap[:, q0:512], ones_b.ap, p_t.ap[:, 0:nq], start=(kb == 0), stop=(kb == nkb - 1))),
                             [ones_b, p_t], [sb_], sig=True)
                        if kb == nkb - 1:
                            ob.w = dict(sb_.w)
                            rs_, of_, zt_, os_ = rs[ai], of[ai], zt[ai], ost[ai]
                            S.op(dve, (lambda: V.reciprocal(rs_.ap, sb_.ap)), [sb_], [rs_])
                            S.op(dve, (lambda: V.tensor_tensor(of_.ap, ob.ap, rs_.ap, ALU.mult)), [ob, rs_], [of_])
                            S.op(pool, (lambda: G.tensor_tensor(os_.ap, of_.ap, zt_.ap, ALU.mult)), [of_, zt_], [os_])
                            S.dma(sp, ogT_s, os_, out_ap=ogT_s.ap[4 * qb:4 * qb + 4, :, head, :].rearrange("i p t -> p i t"),
                                  in_ap=os_.ap.rearrange("p (i t) -> p i t", i=4), sem_tile=os_)

                    for n in range(len(blocks) + LAG):
                        if n < len(blocks):
                            issue_score(n)
                        if n - LAG >= 0:
                            issue_pv(n - LAG)
                S.barrier()
                S.release_dma_sems([maskb] + wqn + wqr + wqp + wuk + wuv + zt + ost)
            S.stack = l1
            outproj_phase(m_wout, x1_s, 1, out_d, pre=wout1_pre)
            S.stack = l1
        S.stack = top
        S.barrier()
        S.emit()
    return nc


def _consts():
    ident = np.eye(128, dtype=np.float32)
    s = np.arange(128)[:, None]
    t = np.arange(128)[None, :]
    W = np.zeros((128, 128), np.float32)
    W[(t < 64) & (s > t) & (s < 64)] = 1.0
    W[(t >= 64) & (s >= 64) & (s <= t)] = -1.0
    U128 = (s > t).astype(np.float32)
    m01 = ((s // 64) <= (t // 64)).astype(np.float32)
    U = np.concatenate([W, U128, m01], axis=1)
    ind = np.zeros((128, 3), np.float32)
    ind[:64, 0] = 1.0
    ind[64:, 1] = 1.0
    ind[:, 2] = 1.0
    mask = np.zeros((128, 4 * 512), np.float32)
    kk = np.arange(128)[:, None]
    qq = np.arange(512)[None, :]
    for j in range(4):
        vis = ((j * 128 + kk) // 64) <= (qq // 64)
        mask[:, j * 512:(j + 1) * 512] = np.where(vis, 0.0, MASKNEG)
    half = 32
    inv = (np.float32(10000.0) ** (-np.arange(half, dtype=np.float32) / np.float32(half))).astype(np.float32)
    rope = np.zeros((128, 2), np.float32)
    p = np.arange(128)
    rope[:, 0] = inv[p % 32]
    rope[:, 1] = np.where((p % 64) < 32, -1.0, 1.0)
    return ident, U, ind, mask, rope


_NC_CACHE = {}


def _prep_shared(gla_w_in, gla_w_a2, gla_b_a, gla_g_out, gla_w_out, mla_w_in, mla_g_q, mla_w_uq, mla_g_kv,
                 mla_w_ukv, mla_w_out, ln_g, ln_b):
    f = lambda a: np.ascontiguousarray(np.asarray(a, dtype=np.float32))
    gw = f(gla_w_in[0])
    g_win = f(gw[:, :6144])
    g_walr = np.zeros((D, 128), np.float32)
    g_walr[:, :16] = gw[:, 6144:6160]
    g_wa2 = np.zeros((128, 1024), np.float32)
    g_wa2[:16] = f(gla_w_a2[0])
    g_wa2[32] = f(gla_b_a[0])
    g_gout = f(np.broadcast_to(f(gla_g_out[0]).reshape(1, 2048), (128, 2048)))
    mw = f(mla_w_in[0])
    m_wc = f(mw[:, :1024])
    kr = mw[:, 1024:1088]
    krp = np.concatenate([kr[:, 32:], kr[:, :32]], axis=1)
    m_wkr = f(np.concatenate([kr, kr, krp, krp], axis=1))
    m_wz = f(mw[:, 1088:])
    m_gcol = f(np.concatenate([f(mla_g_q[0]).reshape(4, 128).T, f(mla_g_kv[0]).reshape(4, 128).T], axis=1))
    uq = f(mla_w_uq[0]).reshape(512, 16, 192)
    m_wuqn = f(uq[:, :, :128].reshape(512, 2048))
    rope = uq[:, :, 128:]
    m_wuqr = f(rope.reshape(512, 1024))
    m_wuqp = f(np.concatenate([rope[:, :, 32:], rope[:, :, :32]], axis=2).reshape(512, 1024))
    ukv = f(mla_w_ukv[0]).reshape(512, 16, 256)
    m_wuk = f(ukv[:, :, :128].reshape(512, 2048))
    m_wuv = f(ukv[:, :, 128:].reshape(512, 2048))
    lng = f(np.broadcast_to(f(ln_g).reshape(1, 2 * D), (128, 2 * D)))
    lnb = f(np.broadcast_to(f(ln_b).reshape(1, 2 * D), (128, 2 * D)))
    ident, U, ind, mask, rope_c = _consts()
    return dict(g_win=g_win, g_walr=g_walr, g_wa2=g_wa2, g_gout=g_gout, g_wout=f(gla_w_out[0]),
                m_wc=m_wc, m_wkr=m_wkr, m_wz=m_wz, m_gcol=m_gcol, m_wuqn=m_wuqn, m_wuqr=m_wuqr, m_wuqp=m_wuqp,
                m_wuk=m_wuk, m_wuv=m_wuv, m_wout=f(mla_w_out[0]), lng_bc=lng, lnb_bc=lnb,
                c_ident=ident, c_U=U, c_ind=ind, c_mask=mask, c_rope=rope_c)


def kernel(x, positions, gla_w_in, gla_w_a2, gla_b_a, gla_g_out, gla_w_out, mla_w_in, mla_g_q, mla_w_uq,
           mla_g_kv, mla_w_ukv, mla_w_out, ln_g, ln_b, _debug=False, _cores=8, _stop=None):
    x = np.asarray(x, dtype=np.float32)
    positions = np.asarray(positions).astype(np.int32)
    shared = _prep_shared(gla_w_in, gla_w_a2, gla_b_a, gla_g_out, gla_w_out, mla_w_in, mla_g_q, mla_w_uq,
                          mla_g_kv, mla_w_ukv, mla_w_out, ln_g, ln_b)
    key = (bool(_debug), _stop)
    if key not in _NC_CACHE:
        _NC_CACHE[key] = build_program(debug=_debug, stop_after=_stop)
    nc = _NC_CACHE[key]
    in_maps = []
    for c in range(_cores):
        m = dict(shared)
        m["x"] = np.ascontiguousarray(x[c])
        m["pos_bc"] = np.ascontiguousarray(np.broadcast_to(positions[c].reshape(1, T), (128, T)))
        in_maps.append(m)
    res = run_bass_kernel_spmd(nc, in_maps, core_ids=list(range(_cores)))
    if _debug:
        return res.results
    return np.stack([np.asarray(r["out"], dtype=np.float32) for r in res.results], axis=0)
```
